# Optimizing a Trainium2 kernel written in Bass

```python
import jax, jax.numpy as jnp
from jax import lax
import numpy as np

D_MODEL = 1024
BATCH = 8
SEQ = 4096
DEPTH = 1

HEAD_DIM = 64
RWKV_HEADS = 8
RWKV_WIDTH = RWKV_HEADS * HEAD_DIM
DECAY_LORA = 64
ICLR_LORA = 64
GATE_LORA = 128
ATTN_Q_HEADS = 8
ATTN_KV_HEADS = 2
ATTN_GROUPS = ATTN_Q_HEADS // ATTN_KV_HEADS
ATTN_Q_WIDTH = ATTN_Q_HEADS * HEAD_DIM
ATTN_KV_WIDTH = ATTN_KV_HEADS * HEAD_DIM
WINDOW = 128
BLOCK = 128
ROPE_THETA = 500000.0
ROPE_DIM = HEAD_DIM // 4
D_FF = -(-8 * D_MODEL // (3 * 256)) * 256
N_BRANCH = 2
RMS_EPS = 1e-6
GN_EPS = 64e-5
NEG_INF = -1e30
RWKV_SHIFT_WIDTH = 3 * RWKV_WIDTH + DECAY_LORA + ICLR_LORA + GATE_LORA
IN_WIDTH = RWKV_SHIFT_WIDTH + ATTN_Q_WIDTH + 2 * ATTN_KV_WIDTH + N_BRANCH * D_MODEL

kernel_name = "hybrid_rwkv7_swa_sink_adaln_block"


def rms_norm(x, gain, eps=RMS_EPS):
    x32 = x.astype(jnp.float32)
    inv = lax.rsqrt(jnp.mean(x32 * x32, axis=-1, keepdims=True) + eps)
    return (x32 * inv).astype(x.dtype) * gain


def token_shift(p):
    return jnp.pad(p, ((0, 0), (1, 0), (0, 0)))[:, :-1]


def partial_rope(x, positions):
    half = ROPE_DIM // 2
    inv_freq = ROPE_THETA ** (-jnp.arange(half, dtype=jnp.float32) / half)
    ang = positions.astype(jnp.float32)[..., None] * inv_freq
    cos = jnp.cos(ang)[:, :, None, :]
    sin = jnp.sin(ang)[:, :, None, :]
    xr = x[..., :ROPE_DIM].astype(jnp.float32)
    x1, x2 = xr[..., :half], xr[..., half:]
    rot = jnp.concatenate([x1 * cos - x2 * sin, x2 * cos + x1 * sin], axis=-1).astype(x.dtype)
    return jnp.concatenate([rot, x[..., ROPE_DIM:]], axis=-1)


def wkv7_scan(r, w, k, v, a, b):
    B_, S_, H, N = r.shape

    def step(state, inp):
        r_t, w_t, k_t, v_t, a_t, b_t = inp
        sa = jnp.einsum('bhvk,bhk->bhv', state, a_t)
        state = state * w_t[:, :, None, :] + sa[..., None] * b_t[:, :, None, :] + v_t[..., None] * k_t[:, :, None, :]
        return state, jnp.einsum('bhvk,bhk->bhv', state, r_t)

    xs = tuple(jnp.moveaxis(t.astype(jnp.float32), 1, 0) for t in (r, w, k, v, a, b))
    init = jnp.zeros((B_, H, N, N), jnp.float32)
    _, ys = lax.scan(step, init, xs)
    return jnp.moveaxis(ys, 0, 1)


def rwkv7_time_mix(cols, decay_w0, decay_up, iclr_a0, iclr_up, gate_up, k_k, k_a, r_k, lnx_gain, lnx_bias):
    B_, S_, _ = cols.shape
    W = RWKV_WIDTH
    f32 = jnp.float32
    r = cols[..., :W]
    k = cols[..., W:2 * W]
    v = cols[..., 2 * W:3 * W]
    o = 3 * W
    xw = cols[..., o:o + DECAY_LORA]
    xa = cols[..., o + DECAY_LORA:o + DECAY_LORA + ICLR_LORA]
    xg = cols[..., o + DECAY_LORA + ICLR_LORA:]
    w_log = -jax.nn.softplus(-(decay_w0 + jnp.tanh(xw) @ decay_up).astype(f32)) - 0.5
    decay = jnp.exp(-jnp.exp(w_log))
    a = jax.nn.sigmoid(iclr_a0 + xa @ iclr_up)
    g = jax.nn.sigmoid(xg) @ gate_up
    heads = lambda t: t.reshape(B_, S_, RWKV_HEADS, HEAD_DIM)
    kk = heads(k * k_k).astype(f32)
    kk = kk / jnp.maximum(jnp.sqrt(jnp.sum(kk * kk, axis=-1, keepdims=True)), 1e-12)
    k = k * (1 + (a - 1) * k_a)
    rh, kh, vh = heads(r), heads(k), heads(v)
    ah = heads(a).astype(f32)
    y = wkv7_scan(rh, heads(decay), kh, vh, -kk, kk * ah)
    mu = jnp.mean(y, axis=-1, keepdims=True)
    var = jnp.mean(jnp.square(y - mu), axis=-1, keepdims=True)
    yn = (y - mu) * lax.rsqrt(var + GN_EPS)
    yn = yn * lnx_gain.reshape(RWKV_HEADS, HEAD_DIM).astype(f32) + lnx_bias.reshape(RWKV_HEADS, HEAD_DIM).astype(f32)
    bonus = jnp.sum((rh * kh * r_k).astype(f32), axis=-1, keepdims=True) * vh.astype(f32)
    return (yn + bonus).reshape(B_, S_, W).astype(cols.dtype) * g


def sliding_window_sink_attention(q, k, v, positions, q_norm_gain, k_norm_gain, sinks):
    B_, S_, _ = q.shape
    nblk = S_ // BLOCK
    q = partial_rope(rms_norm(q.reshape(B_, S_, ATTN_Q_HEADS, HEAD_DIM), q_norm_gain), positions)
    k = partial_rope(rms_norm(k.reshape(B_, S_, ATTN_KV_HEADS, HEAD_DIM), k_norm_gain), positions)
    v = v.reshape(B_, S_, ATTN_KV_HEADS, HEAD_DIM)
    qb = q.reshape(B_, nblk, BLOCK, ATTN_KV_HEADS, ATTN_GROUPS, HEAD_DIM)

    def band(t):
        tb = t.reshape(B_, nblk, BLOCK, ATTN_KV_HEADS, HEAD_DIM)
        prev = jnp.pad(tb, ((0, 0), (1, 0), (0, 0), (0, 0), (0, 0)))[:, :-1]
        return jnp.concatenate([prev, tb], axis=2)

    kband, vband = band(k), band(v)
    scores = jnp.einsum('bnqhgd,bnkhd->bnhgqk', qb, kband).astype(jnp.float32) * (HEAD_DIM ** -0.5)
    q_idx = jnp.arange(BLOCK)[:, None]
    k_idx = jnp.arange(2 * BLOCK)[None, :]
    dist = q_idx + BLOCK - k_idx
    in_band = (dist >= 0) & (dist < WINDOW)
    blk = jnp.arange(nblk)[:, None, None]
    valid = in_band[None] & ((blk > 0) | (k_idx >= BLOCK)[None])
    scores = jnp.where(valid[None, :, None, None], scores, NEG_INF)
    sink = sinks.astype(jnp.float32).reshape(ATTN_KV_HEADS, ATTN_GROUPS)[None, None, :, :, None, None]
    m = jnp.maximum(jnp.max(scores, axis=-1, keepdims=True), sink)
    e = jnp.exp(scores - m)
    probs = e / (jnp.sum(e, axis=-1, keepdims=True) + jnp.exp(sink - m))
    out = jnp.einsum('bnhgqk,bnkhd->bnqhgd', probs.astype(v.dtype), vband)
    return out.reshape(B_, S_, ATTN_Q_WIDTH)


def setup_inputs(seed: int = 0) -> dict:
    key = jax.random.key(seed)
    ks = jax.random.split(key, 32)
    f32 = jnp.float32
    nrm = lambda i, shape, s: jax.random.normal(ks[i], shape, f32) * s
    L = DEPTH
    offsets = jax.random.randint(ks[2], (BATCH, 1), 0, 2048, dtype=jnp.int32)
    positions = offsets + jnp.arange(SEQ, dtype=jnp.int32)[None, :]
    return {
        "x": nrm(0, (BATCH, SEQ, D_MODEL), 1.0),
        "c": nrm(1, (BATCH, D_MODEL), 1.0),
        "positions": positions,
        "ada_w": nrm(3, (L, D_MODEL, 6 * D_MODEL), 0.2 * D_MODEL ** -0.5),
        "ada_b": nrm(4, (L, 6 * D_MODEL), 0.01),
        "norm1_gain": 1.0 + nrm(5, (L, D_MODEL), 0.02),
        "norm2_gain": 1.0 + nrm(6, (L, D_MODEL), 0.02),
        "w_in": nrm(7, (L, D_MODEL, IN_WIDTH), D_MODEL ** -0.5),
        "tshift_mu": jax.random.uniform(ks[8], (L, RWKV_SHIFT_WIDTH), f32),
        "decay_w0": jax.random.uniform(ks[9], (L, RWKV_WIDTH), f32, -6.0, 1.0),
        "decay_up": nrm(10, (L, DECAY_LORA, RWKV_WIDTH), 0.5 * DECAY_LORA ** -0.5),
        "iclr_a0": nrm(11, (L, RWKV_WIDTH), 0.5),
        "iclr_up": nrm(12, (L, ICLR_LORA, RWKV_WIDTH), 0.5 * ICLR_LORA ** -0.5),
        "gate_up": nrm(13, (L, GATE_LORA, RWKV_WIDTH), GATE_LORA ** -0.5),
        "k_k": 0.85 + nrm(14, (L, RWKV_WIDTH), 0.05),
        "k_a": 1.0 + nrm(15, (L, RWKV_WIDTH), 0.05),
        "r_k": nrm(16, (L, RWKV_HEADS, HEAD_DIM), 0.1),
        "lnx_gain": 1.0 + nrm(17, (L, RWKV_WIDTH), 0.02),
        "lnx_bias": nrm(18, (L, RWKV_WIDTH), 0.01),
        "q_norm_gain": 1.0 + nrm(19, (L, HEAD_DIM), 0.02),
        "k_norm_gain": 1.0 + nrm(20, (L, HEAD_DIM), 0.02),
        "attn_sinks": nrm(21, (L, ATTN_Q_HEADS), 1.0),
        "branch_gate_b": nrm(22, (L, N_BRANCH * D_MODEL), 0.1),
        "w_branch_a": nrm(23, (L, RWKV_WIDTH, D_MODEL), RWKV_WIDTH ** -0.5),
        "w_branch_b": nrm(24, (L, ATTN_Q_WIDTH, D_MODEL), ATTN_Q_WIDTH ** -0.5),
        "w_out": nrm(25, (L, D_MODEL, D_MODEL), D_MODEL ** -0.5),
        "ffn_w1": nrm(26, (L, D_MODEL, D_FF), D_MODEL ** -0.5),
        "ffn_w3": nrm(27, (L, D_MODEL, D_FF), D_MODEL ** -0.5),
        "ffn_w2": nrm(28, (L, D_FF, D_MODEL), D_FF ** -0.5),
    }


def reference(x, c, positions, ada_w, ada_b, norm1_gain, norm2_gain, w_in, tshift_mu,
              decay_w0, decay_up, iclr_a0, iclr_up, gate_up, k_k, k_a, r_k, lnx_gain, lnx_bias,
              q_norm_gain, k_norm_gain, attn_sinks, branch_gate_b, w_branch_a, w_branch_b, w_out,
              ffn_w1, ffn_w3, ffn_w2):
    q_lo = RWKV_SHIFT_WIDTH
    k_lo = q_lo + ATTN_Q_WIDTH
    v_lo = k_lo + ATTN_KV_WIDTH
    g_lo = v_lo + ATTN_KV_WIDTH
    for l in range(DEPTH):
        ada = (c @ ada_w[l] + ada_b[l])[:, None, :]
        shift1, scale1, gate1, shift2, scale2, gate2 = jnp.split(ada, 6, axis=-1)

        h = rms_norm(x, norm1_gain[l]) * (1 + scale1) + shift1
        proj = jnp.einsum('bsd,de->bse', h, w_in[l])
        rwkv_cols = proj[..., :q_lo]
        rwkv_cols = rwkv_cols + (token_shift(rwkv_cols) - rwkv_cols) * tshift_mu[l]
        y_a = rwkv7_time_mix(rwkv_cols, decay_w0[l], decay_up[l], iclr_a0[l], iclr_up[l], gate_up[l],
                             k_k[l], k_a[l], r_k[l], lnx_gain[l], lnx_bias[l])
        y_b = sliding_window_sink_attention(proj[..., q_lo:k_lo], proj[..., k_lo:v_lo], proj[..., v_lo:g_lo],
                                            positions, q_norm_gain[l], k_norm_gain[l], attn_sinks[l])
        gates = jax.nn.sigmoid(proj[..., g_lo:] + branch_gate_b[l])
        gate_a, gate_b = gates[..., :D_MODEL], gates[..., D_MODEL:]
        merged = gate_a * (y_a @ w_branch_a[l]) + gate_b * (y_b @ w_branch_b[l])
        x = x + gate1 * (merged @ w_out[l])

        h2 = rms_norm(x, norm2_gain[l]) * (1 + scale2) + shift2
        ffn = (jax.nn.silu(h2 @ ffn_w1[l]) * (h2 @ ffn_w3[l])) @ ffn_w2[l]
        x = x + gate2 * ffn
    return x
```

```python
import numpy as np
from contextlib import ExitStack
import concourse.bass as bass
import concourse.mybir as mybir
from concourse.bass_utils import run_bass_kernel_spmd

F32 = mybir.dt.float32
BF16 = mybir.dt.bfloat16
I32 = mybir.dt.int32
ACT = mybir.ActivationFunctionType
ALU = mybir.AluOpType
AX = mybir.AxisListType

D = 1024
DFF = 2816
NFF = DFF // 128
RMS_EPS = 1e-6
GN_EPS = 64e-5
KDEC = float(0.5 * np.exp(-0.5))
SDT = BF16


class Sched:
    ENG = ("pe", "dve", "act", "pool", "sp")

    def __init__(self, nc, es):
        self.nc = nc
        self.es = es
        self.thunks = {e: [] for e in self.ENG}
        self.sems = {}
        self.count = {}
        self.waited = {e: {} for e in self.ENG}
        self.res = {}
        for e in self.ENG:
            self._mk_sem("e_" + e)
        self.n_ins = 0
        self.n_wait = 0

    def _mk_sem(self, name):
        self.sems[name] = self.es.enter_context(self.nc.semaphore(name))
        self.count[name] = 0

    def _wait(self, eng, s, v):
        W = self.waited[eng]
        if W.get(s, 0) < v:
            W[s] = v
            sem = self.sems[s]
            self.thunks[eng].append(lambda e, sem=sem, v=v: e.wait_ge(sem, v))
            self.n_wait += 1

    def op(self, eng, fn, reads=(), writes=(), chan=None):
        own = "e_" + eng
        need = {}
        for k in reads:
            r = self.res.get(k)
            if r and r["w"]:
                s, v = r["w"]
                need[s] = max(need.get(s, 0), v)
        for k in writes:
            r = self.res.get(k)
            if r:
                if r["w"]:
                    s, v = r["w"]
                    need[s] = max(need.get(s, 0), v)
                for (s, v) in r["r"]:
                    need[s] = max(need.get(s, 0), v)
        for s, v in need.items():
            if s == own and eng == "pe" and chan is None:
                continue
            if not s.startswith("e_"):
                v = self.count[s]
            self._wait(eng, s, v)
        if chan is None:
            s = own
            inc = 1
        else:
            s = chan
            if s not in self.sems:
                self._mk_sem(s)
            inc = 16
        self.count[s] += inc
        tag = (s, self.count[s])
        sem = self.sems[s]
        self.thunks[eng].append(lambda e, fn=fn, sem=sem, inc=inc: fn(e).then_inc(sem, inc))
        self.n_ins += 1
        for k in reads:
            r = self.res.setdefault(k, {"w": None, "r": []})
            r["r"].append(tag)
            if len(r["r"]) > 64:
                best = {}
                for (s2, v2) in r["r"]:
                    best[s2] = max(best.get(s2, 0), v2)
                r["r"] = list(best.items())
        for k in writes:
            self.res[k] = {"w": tag, "r": []}
        return tag

    def barrier(self):
        for eng in self.ENG:
            for s, v in self.count.items():
                if v > 0 and not (s == "e_" + eng):
                    self._wait(eng, s, v)

    def emit(self):
        nc = self.nc
        th = self.thunks
        self.thunks = {e: [] for e in self.ENG}
        with nc.Block() as block:
            @block.tensor
            def _(e):
                for t in th["pe"]:
                    t(e)

            @block.vector
            def _(e):
                for t in th["dve"]:
                    t(e)

            @block.scalar
            def _(e):
                for t in th["act"]:
                    t(e)

            @block.gpsimd
            def _(e):
                for t in th["pool"]:
                    t(e)

            @block.sync
            def _(e):
                for t in th["sp"]:
                    t(e)


def build(S_len=4096, stop_after="B", dbg=False):
    NB = S_len // 128
    nc = bass.Bass("TRN2", target_bir_lowering=False)

    def din(name, shape, dt=F32):
        return nc.dram_tensor(name, list(shape), dt, kind="ExternalInput").ap()

    x_d = din("x", [S_len, D])
    cT_d = din("cT", [128, 8, 2])
    pos_d = din("pos", [128, NB], I32)
    ada_w_d = din("ada_w", [D, 6 * D])
    ada_b_d = din("ada_bT", [128, 48])
    n1g_d = din("n1g", [128, 8])
    n2g_d = din("n2g", [128, 8])
    w_in_d = din("w_in", [D, 4608])
    mu_d = din("mu", [1, 1792])
    w0_d = din("w0T", [128, 4])
    a0_d = din("a0T", [128, 4])
    kk_d = din("kkT", [128, 4])
    ka_d = din("kaT", [128, 4])
    rk_d = din("rkT", [128, 4])
    lora_d = din("lora_up", [128, 512])
    gup_d = din("gate_up", [128, 512])
    lng_d = din("lnx_g", [1, 512])
    lnb_d = din("lnx_b", [1, 512])
    qkg_d = din("qkg", [1, 640])
    sink_d = din("sinks", [1, 8])
    bgb_d = din("bgbT", [128, 16])
    wba_d = din("wba", [512, D])
    wbb_d = din("wbb", [512, D])
    wout_d = din("w_out", [D, D])
    w1_d = din("w1", [D, DFF])
    w3_d = din("w3", [D, DFF])
    w2_d = din("w2", [DFF, D])
    ident_d = din("ident", [128, 128])
    mask1_d = din("mask1", [64, 128])
    masksl_d = din("masksl", [64, 64])
    amask_d = din("amask", [128, 2, 128])
    bones_d = din("bones", [128, 128])
    hind_d = din("hind", [128, 2])
    invf_d = din("invf", [1, 8])

    out_d = nc.dram_tensor("out", [S_len, D], F32, kind="ExternalOutput").ap()
    ysp_kind = "ExternalOutput" if dbg else "Internal"
    ysp_d = nc.dram_tensor("ysp", [2, 128, NB, 4, 128], BF16, kind=ysp_kind).ap()
    dbg_d = {}
    if dbg:
        dbg_d["adaT"] = nc.dram_tensor("dbg_adaT", [128, 48], F32, kind="ExternalOutput").ap()
        dbg_d["ya_tok"] = nc.dram_tensor("dbg_ya_tok", [NB, 2, 64, 512], F32, kind="ExternalOutput").ap()
        dbg_d["yscan"] = nc.dram_tensor("dbg_yscan", [NB, 2, 64, 512], F32, kind="ExternalOutput").ap()
        dbg_d["yb_tok"] = nc.dram_tensor("dbg_yb_tok", [NB, 128, 512], F32, kind="ExternalOutput").ap()

    with ExitStack() as es:
        S = Sched(nc, es)

        def TT(eng, out, in0, in1, op, r, w):
            S.op(eng, lambda e: e.tensor_tensor(out=out, in0=in0, in1=in1, op=op), r, w)

        def TS(eng, out, in0, s1, s2, op0, op1, r, w):
            if s2 is None:
                S.op(eng, lambda e: e.tensor_scalar(out=out, in0=in0, scalar1=s1, scalar2=None, op0=op0), r, w)
            else:
                S.op(eng, lambda e: e.tensor_scalar(out=out, in0=in0, scalar1=s1, scalar2=s2, op0=op0, op1=op1), r, w)

        def STT(out, in0, scalar, in1, op0, op1, r, w):
            S.op("dve", lambda e: e.scalar_tensor_tensor(out=out, in0=in0, scalar=scalar, in1=in1, op0=op0, op1=op1), r, w)

        def ACTF(out, in_, func, r, w, scale=1.0, bias=0.0, accum=None):
            if accum is None:
                S.op("act", lambda e: e.activation(out=out, in_=in_, func=func, bias=bias, scale=scale), r, w)
            else:
                S.op("act", lambda e: e.activation(out=out, in_=in_, func=func, bias=bias, scale=scale, accum_out=accum), r, w)

        def CP(eng, out, in_, r, w):
            if eng == "act":
                S.op("act", lambda e: e.activation(out=out, in_=in_, func=ACT.Copy), r, w)
            else:
                S.op(eng, lambda e: e.tensor_copy(out=out, in_=in_), r, w)

        def MM(out, lhsT, rhs, start, stop, r, w):
            S.op("pe", lambda e: e.matmul(out, lhsT=lhsT, rhs=rhs, start=start, stop=stop), r, w)

        def DMA(eng, out, in_, r, w, chan, **kw):
            if chan == "bulk":
                chan = "bulk_" + eng
            S.op(eng, lambda e: e.dma_start(out=out, in_=in_, **kw), r, w, chan=chan)

        def MEMSET(eng, ap, val, w):
            S.op(eng, lambda e: e.memset(ap, val), (), w)

        ps = [es.enter_context(nc.psum_tensor(f"ps{i}", [128, 512], F32)) for i in range(8)]
        bank_ctr = [0]

        def bank():
            b = bank_ctr[0] % 8
            bank_ctr[0] += 1
            return b

        FRONT_BANKS = [0, 1, 2]
        SCAN_BANKS = [3, 4, 5, 6, 7]
        fctr = [0]
        sctr = [0]

        def fbank():
            b = FRONT_BANKS[fctr[0] % len(FRONT_BANKS)]
            fctr[0] += 1
            return b

        def sbank():
            b = SCAN_BANKS[sctr[0] % len(SCAN_BANKS)]
            sctr[0] += 1
            return b

        def PK(b):
            return f"ps{b}"

        import os as _os
        KSTOP = _os.environ.get("K_STOP", "")

        class _Stop(Exception):
            pass

        def stop_here(tag):
            if KSTOP == tag:
                S.barrier()
                S.emit()
                raise _Stop()

        sb_bytes = {}

        def sb(stack, name, shape, dt=F32):
            n_ = 1
            for d_ in shape[1:]:
                n_ *= d_
            sb_bytes[name] = n_ * (2 if dt == BF16 else 4)
            return stack.enter_context(nc.sbuf_tensor("s_" + name, list(shape), dt))

        ident = sb(es, "ident", [128, 128])
        ones_f = sb(es, "ones_f", [128, 128])
        cm05 = sb(es, "cm05", [128, 128])
        adaT = sb(es, "adaT", [128, 48])
        gs1 = sb(es, "gs1", [128, 8])
        gs2 = sb(es, "gs2", [128, 8])
        hbg = sb(es, "hbg", [128, 16])

        def TR(out, in_, r, w, k=128):
            S.op("pe", lambda e: e.transpose(out=out, in_=in_, identity=ident[0:k, 0:k]), list(r) + ["ident"], w)

        DMA("sp", ident[:], ident_d, (), ["ident"], "c_ident")
        MEMSET("pool", ones_f[:], 1.0, ["ones_f"])
        MEMSET("pool", cm05[:], -0.5, ["cm05"])

        def rsqrt_pool(out, in_, r, w, nparts=128):
            shp = list(in_.shape)
            cv = cm05[0:nparts, 0:1].to_broadcast(shp) if len(shp) == 2 else None
            TT("pool", out, in_, cv, ALU.pow, list(r) + ["cm05"], w)

        with ExitStack() as e1:
            wA = sb(e1, "wA", [128, 8, 2560], BF16)
            wS = sb(e1, "wS", [128, 8, 1792], BF16)
            lora_up = sb(e1, "lora_up", [128, 512])
            gup = sb(e1, "gup", [128, 512], BF16)
            mask1 = sb(e1, "mask1", [64, 128])
            masksl = sb(e1, "masksl", [64, 64])
            amask = sb(e1, "amask", [128, 2, 128], BF16)
            bones = sb(e1, "bones", [128, 128])
            hind = sb(e1, "hind", [128, 2])
            lng_bc = sb(e1, "lng_bc", [64, 512])
            lnb_bc = sb(e1, "lnb_bc", [64, 512])
            qkg_bc = sb(e1, "qkg_bc", [128, 640])
            esink = sb(e1, "esink", [128, 8])
            cos_t = sb(e1, "cos_t", [128, NB, 8])
            sin_t = sb(e1, "sin_t", [128, NB, 8])
            hw0 = sb(e1, "hw0", [128, 4])
            ha0 = sb(e1, "ha0", [128, 4])
            kkT = sb(e1, "kkT", [128, 4])
            hka = sb(e1, "hka", [128, 4])
            omhka = sb(e1, "omhka", [128, 4])
            rkT = sb(e1, "rkT", [128, 4])
            ones64 = sb(e1, "ones64", [128, 64])
            negk = sb(e1, "negk", [128, 1])
            half_c = sb(e1, "half_c", [128, 1])

            with ExitStack() as e1s:
                mu_bc = sb(e1s, "mu_bc", [128, 1792])
                omm_bc = sb(e1s, "omm_bc", [128, 1792])
                stg = [sb(e1s, f"stg{i}", [128, 1792]) for i in range(2)]
                posi = sb(e1s, "posi", [128, NB], I32)
                posf = sb(e1s, "posf", [128, NB])
                invf = sb(e1s, "invf", [128, 8])
                ang = sb(e1s, "ang", [128, NB * 8])
                ki = sb(e1s, "ki", [128, NB * 8], I32)
                kf = sb(e1s, "kf", [128, NB * 8])
                mk_ = sb(e1s, "mk_", [128, NB * 8])
                ang2 = sb(e1s, "ang2", [128, NB * 8])
                t4 = sb(e1s, "t4", [128, 4])
                DMA("sp", mu_bc[:], mu_d.to_broadcast([128, 1792]), (), ["mu_bc"], "c_mu")
                TS("dve", omm_bc[:], mu_bc[:], -1.0, 1.0, ALU.mult, ALU.add, ["mu_bc"], ["omm_bc"])
                for c in range(8):
                    sp_ = c % 2
                    DMA("act", stg[sp_][:], w_in_d[c * 128:(c + 1) * 128, 0:1792], (), [f"stg{sp_}"], f"ld_stg{sp_}")
                    TT("dve", wS[:, c, :], stg[sp_][:], mu_bc[:], ALU.mult, [f"stg{sp_}", "mu_bc"], ["wS"])
                    TT("pool", wA[:, c, 0:1792], stg[sp_][:], omm_bc[:], ALU.mult, [f"stg{sp_}", "omm_bc"], ["wA"])
                    for a in range(2):
                        DMA("pool", wA[:, c, 1792:2304].rearrange("p (h a d) -> p h a d", h=4, a=2)[:, :, a, :],
                            w_in_d[c * 128:(c + 1) * 128, 1792 + a * 256:1792 + (a + 1) * 256].rearrange("p (h d) -> p h d", h=4),
                            (), [f"wA_q{c}_{a}"], "bulk")
                    DMA("pool", wA[:, c, 2304:2560], w_in_d[c * 128:(c + 1) * 128, 2304:2560], (), [f"wA_r{c}"], "bulk")
                DMA("sp", lora_up[:], lora_d, (), ["lora_up"], "bulk")
                DMA("pool", gup[:], gup_d, (), ["gup"], "bulk")
                DMA("sp", mask1[:], mask1_d, (), ["mask1"], "bulk")
                DMA("sp", masksl[:], masksl_d, (), ["masksl"], "bulk")
                DMA("pool", amask[:], amask_d, (), ["amask"], "bulk")
                DMA("sp", bones[:], bones_d, (), ["bones"], "bulk")
                DMA("sp", hind[:], hind_d, (), ["hind"], "bulk")
                DMA("sp", lng_bc[:], lng_d.to_broadcast([64, 512]), (), ["lng_bc"], "bulk")
                DMA("sp", lnb_bc[:], lnb_d.to_broadcast([64, 512]), (), ["lnb_bc"], "bulk")
                DMA("sp", qkg_bc[:], qkg_d.to_broadcast([128, 640]), (), ["qkg_bc"], "c_qkg")
                TS("dve", qkg_bc[:, 0:512], qkg_bc[:, 0:512], 0.125, None, ALU.mult, None, ["qkg_bc"], ["qkg_bc"])
                DMA("sp", esink[:], sink_d.to_broadcast([128, 8]), (), ["esink"], "c_sink")
                ACTF(esink[:], esink[:], ACT.Exp, ["esink"], ["esink"])
                DMA("sp", t4[:], w0_d, (), ["t4"], "c_t4")
                TS("dve", hw0[:], t4[:], 0.5, None, ALU.mult, None, ["t4"], ["hw0"])
                DMA("sp", t4[:], a0_d, (), ["t4"], "c_t4")
                TS("dve", ha0[:], t4[:], 0.5, None, ALU.mult, None, ["t4"], ["ha0"])
                DMA("sp", kkT[:], kk_d, (), ["kkT"], "bulk")
                DMA("sp", rkT[:], rk_d, (), ["rkT"], "bulk")
                DMA("sp", t4[:], ka_d, (), ["t4"], "c_t4")
                TS("dve", hka[:], t4[:], 0.5, None, ALU.mult, None, ["t4"], ["hka"])
                TS("dve", omhka[:], t4[:], -0.5, 1.0, ALU.mult, ALU.add, ["t4"], ["omhka"])
                MEMSET("pool", ones64[:], 1.0, ["ones64"])
                MEMSET("pool", negk[:], -KDEC, ["negk"])
                MEMSET("pool", half_c[:], 0.5, ["half_c"])
                DMA("sp", posi[:], pos_d, (), ["posi"], "c_pos")
                DMA("sp", invf[:], invf_d.to_broadcast([128, 8]), (), ["invf"], "c_invf")
                CP("dve", posf[:], posi[:], ["posi"], ["posf"])
                ang3 = ang[:].rearrange("p (n f) -> p n f", f=8)
                TT("dve", ang3, posf[:].unsqueeze(2).to_broadcast([128, NB, 8]), invf[:].unsqueeze(1).to_broadcast([128, NB, 8]),
                   ALU.mult, ["posf", "invf"], ["ang"])
                TWO_PI = float(2 * np.pi)
                PI = float(np.pi)
                C1 = float(np.float32(6.28125))
                C2 = float(2 * np.pi - 6.28125)

                def wrap(t, key):
                    TS("dve", mk_[:], t, PI, -TWO_PI, ALU.is_gt, ALU.mult, [key], ["mk_"])
                    TT("dve", t, t, mk_[:], ALU.add, [key, "mk_"], [key])
                    TS("dve", mk_[:], t, -PI, TWO_PI, ALU.is_lt, ALU.mult, [key], ["mk_"])
                    TT("dve", t, t, mk_[:], ALU.add, [key, "mk_"], [key])

                TS("dve", ki[:], ang[:], 1.0 / TWO_PI, None, ALU.mult, None, ["ang"], ["ki"])
                CP("dve", kf[:], ki[:], ["ki"], ["kf"])
                STT(ang[:], kf[:], -C1, ang[:], ALU.mult, ALU.add, ["kf", "ang"], ["ang"])
                STT(ang[:], kf[:], -C2, ang[:], ALU.mult, ALU.add, ["kf", "ang"], ["ang"])
                wrap(ang[:], "ang")
                TS("dve", ang2[:], ang[:], PI / 2, None, ALU.add, None, ["ang"], ["ang2"])
                wrap(ang2[:], "ang2")
                TS("dve", ang[:], ang[:], PI, -PI, ALU.min, ALU.max, ["ang"], ["ang"])
                TS("dve", ang2[:], ang2[:], PI, -PI, ALU.min, ALU.max, ["ang2"], ["ang2"])
                ACTF(sin_t[:].rearrange("p n f -> p (n f)"), ang[:], ACT.Sin, ["ang"], ["sin_t"])
                ACTF(cos_t[:].rearrange("p n f -> p (n f)"), ang2[:], ACT.Sin, ["ang2"], ["cos_t"])
                with ExitStack() as e0:
                    cT = sb(e0, "cT", [128, 8, 2])
                    adab = sb(e0, "adab", [128, 48])
                    n1g = sb(e0, "n1g", [128, 8])
                    n2g = sb(e0, "n2g", [128, 8])
                    bgb = sb(e0, "bgb", [128, 16])
                    tmp8 = sb(e0, "tmp8", [128, 8])
                    adarow = sb(e0, "adarow", [2, 6 * D])
                    awp = [sb(e0, f"awp{i}", [128, 8, 1024]) for i in range(2)]
                    DMA("sp", cT[:], cT_d, (), ["cT"], "c_cT")
                    DMA("sp", adab[:], ada_b_d, (), ["adab"], "c_adab")
                    DMA("sp", n1g[:], n1g_d, (), ["n1g"], "c_n1g")
                    DMA("sp", n2g[:], n2g_d, (), ["n2g"], "c_n2g")
                    DMA("sp", bgb[:], bgb_d, (), ["bgb"], "c_bgb")
                    for g in range(6):
                        pb = g % 2
                        for kc in range(8):
                            DMA("sp", awp[pb][:, kc, :], ada_w_d[kc * 128:(kc + 1) * 128, g * 1024:(g + 1) * 1024],
                                (), [f"awp{pb}_{kc}"], f"ld_awp{pb}")
                        AW = [f"awp{pb}_{k_}" for k_ in range(8)]
                        for hb in range(2):
                            bR = bank()
                            for kc in range(8):
                                MM(ps[bR][0:2, :], cT[:, kc, :], awp[pb][:, kc, hb * 512:(hb + 1) * 512], kc == 0, kc == 7, AW + ["cT"], [PK(bR)])
                            CP("act", adarow[:, g * 1024 + hb * 512:g * 1024 + (hb + 1) * 512], ps[bR][0:2, :], [PK(bR)], ["adarow"])
                    bA = bank()
                    for j in range(48):
                        S.op("pe", lambda e, j=j: e.transpose(out=ps[bA][:, j:j + 1], in_=adarow[0:1, j * 128:(j + 1) * 128], identity=ident[0:1, 0:1]),
                             ["adarow", "ident"], [PK(bA)])
                    TT("dve", adaT[:], ps[bA][:, 0:48], adab[:], ALU.add, [PK(bA), "adab"], ["adaT"])
                    TS("dve", tmp8[:], adaT[:, 8:16], 1.0, None, ALU.add, None, ["adaT"], ["tmp8"])
                    TT("dve", gs1[:], tmp8[:], n1g[:], ALU.mult, ["tmp8", "n1g"], ["gs1"])
                    TS("dve", tmp8[:], adaT[:, 32:40], 1.0, None, ALU.add, None, ["adaT"], ["tmp8"])
                    TT("dve", gs2[:], tmp8[:], n2g[:], ALU.mult, ["tmp8", "n2g"], ["gs2"])
                    TS("dve", hbg[:], bgb[:], 0.5, None, ALU.mult, None, ["bgb"], ["hbg"])
                    if dbg:
                        DMA("sp", dbg_d["adaT"], adaT[:], ["adaT"], (), "dbg_adaT")
                    S.barrier()
                    S.emit()
                if KSTOP == "S1":
                    return nc

            xa = [sb(e1, f"xa{i}", [128, D]) for i in range(2)]
            xn = sb(e1, "xn", [128, D])
            ss = sb(e1, "ss", [128, 4])
            hT = [sb(e1, f"hT{i}", [128, 8, 129], BF16) for i in range(2)]
            rT = sb(e1, "rT", [128, 4, 128])
            kT = sb(e1, "kT", [128, 4, 128])
            txw = sb(e1, "txw", [128, 128])
            sgT2 = [sb(e1, f"sgT{i}", [128, 128], BF16) for i in range(2)]
            tg = sb(e1, "tg", [128, 128])
            vtok2 = [sb(e1, f"vtok_{i}", [64, 2, 512]) for i in range(2)]
            qkv = sb(e1, "qkv", [128, 768])
            sqt = sb(e1, "sqt", [128, 640])
            ssq = sb(e1, "ssq", [128, 16])
            rq = sb(e1, "rq", [128, 16])
            rt4 = sb(e1, "rt4", [128, 4, 80])
            qT = sb(e1, "qT", [128, 4, 128], BF16)
            kTb = sb(e1, "kTb", [128, 2, 128], BF16)
            vaug = sb(e1, "vaug", [128, 2, 2, 65], BF16)
            eT = sb(e1, "eT", [128, 2, 512], BF16)
            den = sb(e1, "den", [128, 8])
            yb_tok = sb(e1, "yb_tok", [128, 512])
            ybT = sb(e1, "ybT", [128, 4, 128], BF16)
            tw = sb(e1, "tw", [128, 128])
            lw = sb(e1, "lw", [128, 128])
            ta4 = sb(e1, "ta4", [128, 4, 128])
            kk2 = sb(e1, "kk2", [128, 128])
            ssk4 = sb(e1, "ssk4", [128, 4, 128])
            kkn = sb(e1, "kkn", [128, 128])
            t1 = sb(e1, "t1", [128, 128])
            kp = sb(e1, "kp", [128, 128])
            a_ = sb(e1, "a_", [128, 128])
            bb = sb(e1, "bb", [128, 128])
            Lc = sb(e1, "Lc", [128, 128])
            enL4 = sb(e1, "enL4", [128, 4, 128])
            rk4 = sb(e1, "rk4", [128, 4, 128])
            AR2 = [[sb(e1, f"AR{c}_{p}", [128, 2, 2, 64], SDT) for c in range(4)] for p in range(2)]
            btT2 = [[sb(e1, f"btT{c}_{p}", [128, 128], SDT) for c in range(4)] for p in range(2)]
            ktT2 = [[sb(e1, f"ktT{c}_{p}", [128, 128], SDT) for c in range(4)] for p in range(2)]
            BbT2 = [[sb(e1, f"BbT{c}_{p}", [128, 128]) for c in range(4)] for p in range(2)]
            KbT2 = [[sb(e1, f"KbT{c}_{p}", [128, 128]) for c in range(4)] for p in range(2)]
            eLx2 = [[sb(e1, f"eLx{c}_{p}", [128, 2, 65]) for c in range(4)] for p in range(2)]
            bon2 = [sb(e1, f"bon_{p}", [64, 2, 8]) for p in range(2)]
            A1s2 = [sb(e1, f"A1s_{j}", [64, 8, 128], SDT) for j in range(2)]
            A2s2 = [sb(e1, f"A2s_{j}", [64, 8, 128], SDT) for j in range(2)]
            Pm2 = [sb(e1, f"Pm_{j}", [64, 8, 64], SDT) for j in range(2)]
            PTm2 = [sb(e1, f"PTm_{j}", [64, 8, 64], SDT) for j in range(2)]
            TTm2 = [sb(e1, f"TTm_{j}", [64, 8, 64], SDT) for j in range(2)]
            Xs = sb(e1, "Xs", [64, 8, 64], SDT)
            Us = sb(e1, "Us", [64, 8, 64], SDT)
            Btok2 = [sb(e1, f"Btok_{j}", [64, 4, 128], SDT) for j in range(2)]
            Ktok2 = [sb(e1, f"Ktok_{j}", [64, 4, 128], SDT) for j in range(2)]
            Hb = sb(e1, "Hb", [128, 4, 128], SDT)
            vtokb2 = [sb(e1, f"vtokb_{i}", [64, 2, 512], SDT) for i in range(2)]
            Hblk = sb(e1, "Hblk", [128, 4, 128])
            ysb2 = [sb(e1, f"ysb_{j}", [64, 512]) for j in range(2)]
            ysq = sb(e1, "ysq", [64, 512])
            yc = sb(e1, "yc", [64, 512])
            bv = sb(e1, "bv", [64, 512])
            s1 = sb(e1, "s1", [64, 8])
            s2 = sb(e1, "s2", [64, 8])
            mean = sb(e1, "mean", [64, 8])
            msq = sb(e1, "msq", [64, 8])
            var = sb(e1, "var", [64, 8])
            rstd = sb(e1, "rstd", [64, 8])
            ya_tok = sb(e1, "ya_tok", [64, 512])
            yaT = sb(e1, "yaT", [128, 4, 128], BF16)

            MEMSET("pool", Hblk[:], 0.0, ["Hst"])
            MEMSET("pool", Hb[:], 0.0, ["Hb"])
            MEMSET("pool", vaug[:], 1.0, ["vaug"])
            for p_ in range(2):
                for c in range(4):
                    MEMSET("pool", eLx2[p_][c][:], 1.0, [f"eLx{c}_{p_}"])
            MEMSET("pool", hT[0][:], 0.0, ["hT0"])
            MEMSET("pool", hT[1][:], 0.0, ["hT1"])

            def front(n):
                par = n % 2
                slot, pslot = n % 2, (n + 1) % 2
                XA, HT = f"xa{par}", f"hT{par}"
                vtok, vtokb, sgT = vtok2[par], vtokb2[par], sgT2[par]
                VK = f"_{par}"
                DMA("sp", xa[par][:], x_d[n * 128:(n + 1) * 128, :], (), [XA], f"ldx{par}")
                ACTF(xn[:], xa[par][:], ACT.Square, [XA], ["xn", "ss"], accum=ss[:, 0:1])
                TS("dve", ss[:, 1:2], ss[:, 0:1], 1.0 / D, RMS_EPS, ALU.mult, ALU.add, ["ss"], ["ss1"])
                rsqrt_pool(ss[:, 2:3], ss[:, 1:2], ["ss1"], ["ss2"])
                ACTF(xn[:], xa[par][:], ACT.Identity, [XA, "ss2", "ss"], ["xn"], scale=ss[:, 2:3])
                yield
                if n > 0:
                    CP("pool", hT[par][:, :, 0:1], hT[1 - par][:, :, 128:129], [f"hT{1 - par}"], [HT])
                for half in range(2):
                    b = fbank()
                    for q in range(4):
                        c = half * 4 + q
                        TR(ps[b][:, q * 128:(q + 1) * 128], xn[:, c * 128:(c + 1) * 128], ["xn"], [PK(b)])
                    for q in range(4):
                        c = half * 4 + q
                        if q % 2 == 0:
                            ACTF(hT[par][:, c, 1:129], ps[b][:, q * 128:(q + 1) * 128], ACT.Identity, [PK(b), "gs1", "adaT"], [HT],
                                 scale=gs1[:, c:c + 1], bias=adaT[:, c:c + 1])
                        else:
                            TS("dve", hT[par][:, c, 1:129], ps[b][:, q * 128:(q + 1) * 128], gs1[:, c:c + 1], adaT[:, c:c + 1],
                               ALU.mult, ALU.add, [PK(b), "gs1", "adaT"], [HT])

                yield
                def fm_group(col0, out):
                    b_ = out[0]
                    for c in range(8):
                        MM(out[1], wA[:, c, col0:col0 + 128], hT[par][:, c, 1:129], c == 0, False, ["wA", HT], [PK(b_)])
                    for c in range(8):
                        MM(out[1], wS[:, c, col0:col0 + 128], hT[par][:, c, 0:128], False, c == 7, ["wS", HT], [PK(b_)])

                b = fbank()
                for q in range(4):
                    fm_group(q * 128, (b, ps[b][:, q * 128:(q + 1) * 128]))
                    yield
                CP("act", rT[:].rearrange("p c t -> p (c t)"), ps[b][:, :], [PK(b)], ["rT"])
                yield
                b = fbank()
                for q in range(4):
                    fm_group(512 + q * 128, (b, ps[b][:, q * 128:(q + 1) * 128]))
                    yield
                CP("dve", kT[:].rearrange("p c t -> p (c t)"), ps[b][:, :], [PK(b)], ["kT"])
                yield
                b = fbank()
                fm_group(1536, (b, ps[b][:, 0:128]))
                yield
                fm_group(1664, (b, ps[b][:, 128:256]))
                ACTF(txw[0:64, :], ps[b][0:64, 0:128], ACT.Tanh, [PK(b)], ["txw"])
                ACTF(txw[64:128, :], ps[b][64:128, 0:128], ACT.Identity, [PK(b)], ["txw"])
                ACTF(tg[:], ps[b][:, 128:256], ACT.Tanh, [PK(b)], ["tg"], scale=0.5)
                TS("dve", sgT[:], tg[:], 0.5, 0.5, ALU.mult, ALU.add, ["tg"], ["sgT" + VK])

                yield
                for j in range(2):
                    b = fbank()
                    for c in range(8):
                        MM(ps[b][0:64, :], hT[par][:, c, 1 + 64 * j:65 + 64 * j], wA[:, c, 1024:1536], c == 0, False, ["wA", HT], [PK(b)])
                    for c in range(8):
                        MM(ps[b][0:64, :], hT[par][:, c, 64 * j:64 * j + 64], wS[:, c, 1024:1536], False, c == 7, ["wS", HT], [PK(b)])
                    CP("act", vtok[:, j, :], ps[b][0:64, :], [PK(b)], [f"vtok{j}" + VK])
                    CP("dve", vtokb[:, j, :], vtok[:, j, :], [f"vtok{j}" + VK], [f"vtokb{j}" + VK])
                    yield
                b = fbank()
                for c in range(8):
                    MM(ps[b][:, :], hT[par][:, c, 1:129], wA[:, c, 1792:2304], c == 0, c == 7, ["wA", HT], [PK(b)])
                CP("act", qkv[:, 0:512], ps[b][:, :], [PK(b)], ["qkv"])
                yield
                b = fbank()
                for c in range(8):
                    MM(ps[b][:, 0:256], hT[par][:, c, 1:129], wA[:, c, 2304:2560], c == 0, c == 7, ["wA", HT], [PK(b)])
                CP("dve", qkv[:, 512:768], ps[b][:, 0:256], [PK(b)], ["qkv"])

                yield
                qk3 = qkv[:, 0:640].rearrange("p (h d) -> p h d", d=64)
                ACTF(sqt[:], qkv[:, 0:640], ACT.Square, ["qkv"], ["sqt"])
                S.op("dve", lambda e: e.tensor_reduce(out=ssq[:, 0:10], in_=sqt[:].rearrange("p (h d) -> p h d", d=64), axis=AX.X, op=ALU.add),
                     ["sqt"], ["ssq"])
                TS("dve", ssq[:, 0:10], ssq[:, 0:10], 1.0 / 64, RMS_EPS, ALU.mult, ALU.add, ["ssq"], ["ssq"])
                TT("pool", rq[:, 0:10], ssq[:, 0:10], cm05[:, 0:10], ALU.pow, ["ssq", "cm05"], ["rq"])
                TT("dve", qk3, qk3, rq[:, 0:10].unsqueeze(2).to_broadcast([128, 10, 64]), ALU.mult, ["qkv", "rq"], ["qkv"])
                TT("dve", qkv[:, 0:640], qkv[:, 0:640], qkg_bc[:], ALU.mult, ["qkv", "qkg_bc"], ["qkv"])
                x1v, x2v = qk3[:, :, 0:8], qk3[:, :, 8:16]
                cosb = cos_t[:, n, :].unsqueeze(1).to_broadcast([128, 10, 8])
                sinb = sin_t[:, n, :].unsqueeze(1).to_broadcast([128, 10, 8])
                r4 = [rt4[:, i, :].rearrange("p (h f) -> p h f", f=8) for i in range(4)]
                TT("dve", r4[0], x1v, cosb, ALU.mult, ["qkv", "cos_t"], ["rt4_0"])
                TT("dve", r4[1], x2v, sinb, ALU.mult, ["qkv", "sin_t"], ["rt4_1"])
                TT("dve", r4[2], x2v, cosb, ALU.mult, ["qkv", "cos_t"], ["rt4_2"])
                TT("dve", r4[3], x1v, sinb, ALU.mult, ["qkv", "sin_t"], ["rt4_3"])
                TT("dve", x1v, r4[0], r4[1], ALU.subtract, ["rt4_0", "rt4_1"], ["qkv"])
                TT("dve", x2v, r4[2], r4[3], ALU.add, ["rt4_2", "rt4_3"], ["qkv"])
                yield
                b = fbank()
                for h in range(4):
                    TR(ps[b][:, h * 128:(h + 1) * 128], qkv[:, h * 128:(h + 1) * 128], ["qkv"], [PK(b)])
                CP("act", qT[:].rearrange("p h t -> p (h t)"), ps[b][:, :], [PK(b)], ["qT"])
                b = fbank()
                TR(ps[b][:, 0:128], qkv[:, 512:640], ["qkv"], [PK(b)])
                CP("dve", kTb[:, slot, :], ps[b][:, 0:128], [PK(b)], [f"kTb{slot}"])
                CP("pool", vaug[:, slot, :, 0:64], qkv[:, 640:768].rearrange("p (g d) -> p g d", d=64), ["qkv"], [f"vaug{slot}"])

                yield
                for g in range(2):
                    pr = slice(64 * g, 64 * g + 64)
                    bc_ = fbank()
                    MM(ps[bc_][:, :], kTb[pr, slot, :], qT[pr, :, :], True, True, [f"kTb{slot}", "qT"], [PK(bc_)])
                    ACTF(eT[:, 1, :], ps[bc_][:, :], ACT.Exp, [PK(bc_)], ["eT1"])
                    TT("dve", eT[:, 1, :].rearrange("p (h t) -> p h t", h=4), eT[:, 1, :].rearrange("p (h t) -> p h t", h=4),
                       amask[:, 1, :].unsqueeze(1).to_broadcast([128, 4, 128]), ALU.mult, ["eT1", "amask"], ["eT1"])
                    if n > 0:
                        bp_ = fbank()
                        MM(ps[bp_][:, :], kTb[pr, pslot, :], qT[pr, :, :], True, True, [f"kTb{pslot}", "qT"], [PK(bp_)])
                        ACTF(eT[:, 0, :], ps[bp_][:, :], ACT.Exp, [PK(bp_)], ["eT0"])
                        TT("dve", eT[:, 0, :].rearrange("p (h t) -> p h t", h=4), eT[:, 0, :].rearrange("p (h t) -> p h t", h=4),
                           amask[:, 0, :].unsqueeze(1).to_broadcast([128, 4, 128]), ALU.mult, ["eT0", "amask"], ["eT0"])
                    yield
                    bo = fbank()
                    for jq in range(4):
                        o = ps[bo][:, jq * 65:(jq + 1) * 65]
                        if n > 0:
                            MM(o, eT[:, 0, jq * 128:(jq + 1) * 128], vaug[:, pslot, g, :], True, False, ["eT0", f"vaug{pslot}"], [PK(bo)])
                        MM(o, eT[:, 1, jq * 128:(jq + 1) * 128], vaug[:, slot, g, :], n == 0, True, ["eT1", f"vaug{slot}"], [PK(bo)])
                    pv3 = ps[bo][:, 0:260].rearrange("p (j e) -> p j e", e=65)
                    TT("dve", den[:, g * 4:(g + 1) * 4], pv3[:, :, 64], esink[:, g * 4:(g + 1) * 4], ALU.add, [PK(bo), "esink"], ["den"])
                    S.op("dve", lambda e, g=g: e.reciprocal(out=den[:, g * 4:(g + 1) * 4], in_=den[:, g * 4:(g + 1) * 4]), ["den"], ["den"])
                    TT("dve", yb_tok[:, g * 256:(g + 1) * 256].rearrange("p (j d) -> p j d", d=64), pv3[:, :, 0:64],
                       den[:, g * 4:(g + 1) * 4].unsqueeze(2).to_broadcast([128, 4, 64]), ALU.mult, [PK(bo), "den"], ["yb_tok"])
                yield
                if dbg:
                    DMA("sp", dbg_d["yb_tok"][n], yb_tok[:], ["yb_tok"], (), "dbg_yb")
                b = fbank()
                for c in range(4):
                    TR(ps[b][:, c * 128:(c + 1) * 128], yb_tok[:, c * 128:(c + 1) * 128], ["yb_tok"], [PK(b)])
                CP("act", ybT[:].rearrange("p c t -> p (c t)"), ps[b][:, :], [PK(b)], ["ybT"])
                DMA("sp", ysp_d[1, :, n], ybT[:], ["ybT"], (), "st_ybT")

                yield

            def stage8(n):
                par = n % 2
                AR, btT, ktT, BbT, KbT, eLx, bon = AR2[par], btT2[par], ktT2[par], BbT2[par], KbT2[par], eLx2[par], bon2[par]
                PS = f"_{par}"
                for c in range(4):
                    cs = slice(c * 128, (c + 1) * 128)
                    b, bq = fbank(), fbank()
                    MM(ps[b][:, 0:128], lora_up[0:64, cs], txw[0:64, :], True, True, ["lora_up", "txw"], [PK(b)])
                    MM(ps[bq][:, 128:256], lora_up[64:128, cs], txw[64:128, :], True, True, ["lora_up", "txw"], [PK(bq)])
                    ACTF(tw[:], ps[b][:, 0:128], ACT.Tanh, [PK(b), "hw0"], ["tw"], scale=0.5, bias=hw0[:, c:c + 1])
                    ACTF(ta4[:, c, :], ps[bq][:, 128:256], ACT.Tanh, [PK(bq), "ha0"], [f"ta4_{c}"], scale=0.5, bias=ha0[:, c:c + 1])
                    ACTF(lw[:], tw[:], ACT.Identity, ["tw"], ["lw"], scale=-KDEC, bias=negk[:, 0:1])
                    yield
                    ACTF(kk2[:], kT[:, c, :], ACT.Square, ["kT", "kkT"], ["kk2"], scale=kkT[:, c:c + 1])
                    b2 = fbank()
                    MM(ps[b2][:, 0:128], bones[:], kk2[:], True, True, ["bones", "kk2"], [PK(b2)])
                    TS("dve", ssk4[:, c, :], ps[b2][:, 0:128], 1e-24, None, ALU.add, None, [PK(b2)], [f"ssk4_{c}"])
                    ACTF(t1[:], ta4[:, c, :], ACT.Identity, [f"ta4_{c}", "hka", "omhka"], ["t1"], scale=hka[:, c:c + 1], bias=omhka[:, c:c + 1])
                    TT("pool", kp[:], t1[:], kT[:, c, :], ALU.mult, ["t1", "kT"], ["kp"])
                    for j in range(2):
                        js = slice(j * 64, (j + 1) * 64)
                        S.op("dve", lambda e, js=js: e.tensor_tensor_scan(out=Lc[:, js], data0=ones64[:], data1=lw[:, js], initial=0.0,
                                                                         op0=ALU.mult, op1=ALU.add), ["lw", "ones64"], ["Lc"])
                    yield
                    L3 = Lc[:].rearrange("p (j t) -> p j t", t=64)
                    ACTF(eLx[c][:, :, 1:65], L3, ACT.Exp, ["Lc"], [f"eLx{c}" + PS])
                    ACTF(enL4[:, c, :], Lc[:], ACT.Exp, ["Lc"], [f"enL4_{c}"], scale=-1.0)
                    TT("dve", AR[c][:, :, 1, :], rT[:, c, :].rearrange("p (j t) -> p j t", t=64), eLx[c][:, :, 1:65], ALU.mult,
                       ["rT", f"eLx{c}" + PS], [f"AR{c}" + PS])
                    TT("pool", ktT[c][:], kp[:], enL4[:, c, :], ALU.mult, ["kp", f"enL4_{c}"], [f"ktT{c}" + PS])
                    for j in range(2):
                        js = slice(j * 64, (j + 1) * 64)
                        ACTF(KbT[c][:, js], ktT[c][:, js], ACT.Identity, [f"ktT{c}" + PS, f"eLx{c}" + PS], [f"KbT{c}" + PS], scale=eLx[c][:, j, 64:65])
                    STT(rk4[:, c, :], rT[:, c, :], rkT[:, c:c + 1], kp[:], ALU.mult, ALU.mult, ["rT", "rkT", "kp"], ["rk4"])
                    yield
                ssk_flat = ssk4[:].rearrange("p c t -> p (c t)")
                SSK = [f"ssk4_{c}" for c in range(4)]
                ACTF(ssk_flat, ssk_flat, ACT.Ln, SSK, SSK)
                ACTF(ssk_flat, ssk_flat, ACT.Exp, SSK, SSK, scale=-0.5)
                for c in range(4):
                    STT(kkn[:], kT[:, c, :], kkT[:, c:c + 1], ssk4[:, c, :], ALU.mult, ALU.mult, ["kT", "kkT", f"ssk4_{c}"], ["kkn"])
                    ACTF(a_[:], ta4[:, c, :], ACT.Identity, [f"ta4_{c}"], ["a_"], scale=0.5, bias=half_c[:, 0:1])
                    TT("pool", bb[:], a_[:], kkn[:], ALU.mult, ["a_", "kkn"], ["bb"])
                    STT(AR[c][:, :, 0, :], kkn[:].rearrange("p (j t) -> p j t", t=64), -1.0, eLx[c][:, :, 0:64], ALU.mult, ALU.mult,
                        ["kkn", f"eLx{c}" + PS], [f"AR{c}" + PS])
                    TT("pool", btT[c][:], bb[:], enL4[:, c, :], ALU.mult, ["bb", f"enL4_{c}"], [f"btT{c}" + PS])
                    for j in range(2):
                        js = slice(j * 64, (j + 1) * 64)
                        ACTF(BbT[c][:, js], btT[c][:, js], ACT.Identity, [f"btT{c}" + PS, f"eLx{c}" + PS], [f"BbT{c}" + PS], scale=eLx[c][:, j, 64:65])
                    yield
                bB = fbank()
                for c in range(4):
                    for j in range(2):
                        MM(ps[bB][0:64, j * 8 + 2 * c:j * 8 + 2 * c + 2], rk4[:, c, j * 64:(j + 1) * 64], hind[:], True, True, ["rk4", "hind"], [PK(bB)])
                CP("act", bon[:].rearrange("p j h -> p (j h)"), ps[bB][0:64, 0:16], [PK(bB)], ["bon" + PS])

                yield

            def scan_post(n):
                par = n % 2
                vtok, vtokb, sgT = vtok2[par], vtokb2[par], sgT2[par]
                AR, btT, ktT, BbT, KbT, eLx, bon = AR2[par], btT2[par], ktT2[par], BbT2[par], KbT2[par], eLx2[par], bon2[par]
                VK = f"_{par}"
                PS = f"_{par}"

                def hp(h):
                    return h // 2, slice(64 * (h % 2), 64 * (h % 2) + 64)

                def slot_(h):
                    return (h % 2) * 4 + h // 2

                m1b = mask1[:].unsqueeze(1).to_broadcast([64, 4, 128])
                for j in range(2):
                    js = slice(j * 64, (j + 1) * 64)
                    A1s, A2s, Pm, TTm = A1s2[j], A2s2[j], Pm2[j], TTm2[j]
                    A1K, A2K, PK_, TK = f"A1s{j}", f"A2s{j}", f"Pm{j}", f"TTm{j}"
                    bA1 = [sbank(), sbank()]
                    for h in range(8):
                        c, pr = hp(h)
                        MM(ps[bA1[h % 2]][0:64, (h // 2) * 128:(h // 2 + 1) * 128], btT[c][pr, js], AR[c][pr, j, :, :], True, True,
                           [f"btT{c}" + PS, f"AR{c}" + PS], [PK(bA1[h % 2])])
                    for q in range(2):
                        TT("dve", A1s[:, q * 4:(q + 1) * 4, :], ps[bA1[q]][0:64, :].rearrange("p (h t) -> p h t", h=4), m1b, ALU.mult,
                           [PK(bA1[q]), "mask1"], [A1K])
                    yield
                    bA2 = [sbank(), sbank()]
                    for h in range(8):
                        c, pr = hp(h)
                        MM(ps[bA2[h % 2]][0:64, (h // 2) * 128:(h // 2 + 1) * 128], ktT[c][pr, js], AR[c][pr, j, :, :], True, True,
                           [f"ktT{c}" + PS, f"AR{c}" + PS], [PK(bA2[h % 2])])
                    for q in range(2):
                        TT("dve", A2s[:, q * 4:(q + 1) * 4, :], ps[bA2[q]][0:64, :].rearrange("p (h t) -> p h t", h=4), m1b, ALU.mult,
                           [PK(bA2[q]), "mask1"], [A2K])
                    yield
                    bA3 = [sbank(), sbank()]
                    for h in range(8):
                        c, pr = hp(h)
                        MM(ps[bA3[h % 2]][0:64, (h // 2) * 64:(h // 2 + 1) * 64], AR[c][pr, j, 0, :], btT[c][pr, js], True, True,
                           [f"btT{c}" + PS, f"AR{c}" + PS], [PK(bA3[h % 2])])
                    for q in range(2):
                        TT("dve", Pm[:, q * 4:(q + 1) * 4, :], ps[bA3[q]][0:64, 0:256].rearrange("p (h t) -> p h t", h=4),
                           masksl[:].unsqueeze(1).to_broadcast([64, 4, 64]), ALU.mult, [PK(bA3[q]), "masksl"], [PK_])
                    TT("pool", TTm[:], A1s[:, :, 0:64], ident[0:64, 0:64].unsqueeze(1).to_broadcast([64, 8, 64]), ALU.add,
                       [A1K, "ident"], [TK])
                    yield
                    bBt, bKt = sbank(), sbank()
                    for c in range(4):
                        TR(ps[bBt][0:64, c * 128:(c + 1) * 128], BbT[c][:, js], [f"BbT{c}" + PS], [PK(bBt)])
                    for c in range(4):
                        TR(ps[bKt][0:64, c * 128:(c + 1) * 128], KbT[c][:, js], [f"KbT{c}" + PS], [PK(bKt)])
                    CP("act", Btok2[j][:].rearrange("p c t -> p (c t)"), ps[bBt][0:64, :], [PK(bBt)], [f"Btok{j}"])
                    CP("act", Ktok2[j][:].rearrange("p c t -> p (c t)"), ps[bKt][0:64, :], [PK(bKt)], [f"Ktok{j}"])
                    yield
                PTcur = [A1s2[0][:, :, 0:64], A1s2[1][:, :, 0:64]]
                PTkey = ["A1s0", "A1s1"]
                for l in range(1, 6):
                    for j in range(2):
                        Pm, PTm, TTm = Pm2[j], PTm2[j], TTm2[j]
                        PK_, PTK, TK = f"Pm{j}", f"PTm{j}", f"TTm{j}"
                        bP = sbank()
                        for h in range(8):
                            MM(ps[bP][0:64, h * 64:(h + 1) * 64], PTcur[j][:, h, :], Pm[:, h, :], True, True, [PTkey[j], PK_], [PK(bP)])
                        if l < 5:
                            bPT = sbank()
                            for h in range(8):
                                MM(ps[bPT][0:64, h * 64:(h + 1) * 64], Pm[:, h, :], PTcur[j][:, h, :], True, True, [PTkey[j], PK_], [PK(bPT)])
                        CP("act", Pm[:].rearrange("p h t -> p (h t)"), ps[bP][0:64, :], [PK(bP)], [PK_])
                        if l < 5:
                            CP("act", PTm[:].rearrange("p h t -> p (h t)"), ps[bPT][0:64, :], [PK(bPT)], [PTK])
                            PTcur[j], PTkey[j] = PTm[:], PTK
                        yield
                    for j in range(2):
                        Pm, TTm = Pm2[j], TTm2[j]
                        PK_, TK = f"Pm{j}", f"TTm{j}"
                        bT = sbank()
                        for h in range(8):
                            MM(ps[bT][0:64, h * 64:(h + 1) * 64], Pm[:, h, :], TTm[:, h, :], True, True, [PK_, TK], [PK(bT)])
                        TT("dve", TTm[:].rearrange("p h t -> p (h t)"), TTm[:].rearrange("p h t -> p (h t)"), ps[bT][0:64, :], ALU.add,
                           [TK, PK(bT)], [TK])
                        yield
                for j in range(2):
                    js = slice(j * 64, (j + 1) * 64)
                    VT = f"vtok{j}" + VK
                    VTB = f"vtokb{j}" + VK
                    A1s, A2s, TTm, Btok, Ktok, ysb = A1s2[j], A2s2[j], TTm2[j], Btok2[j], Ktok2[j], ysb2[j]
                    A1K, A2K, TK, BK, KK, YK = f"A1s{j}", f"A2s{j}", f"TTm{j}", f"Btok{j}", f"Ktok{j}", f"ysb{j}"
                    bX = sbank()
                    for c in range(4):
                        o = ps[bX][0:64, c * 128:(c + 1) * 128]
                        MM(o, AR[c][:, j, 0, :], Hb[:, c, :], True, False, [f"AR{c}" + PS, "Hb"], [PK(bX)])
                        for i in range(2):
                            h = 2 * c + i
                            MM(ps[bX][0:64, h * 64:(h + 1) * 64], A2s[:, slot_(h), 0:64], vtokb[:, j, h * 64:(h + 1) * 64], False, i == 1,
                               [A2K, VTB], [PK(bX)])
                    CP("act", Xs[:].rearrange("p h t -> p (h t)"), ps[bX][0:64, :], [PK(bX)], ["Xs"])
                    yield
                    bU = sbank()
                    for h in range(8):
                        MM(ps[bU][0:64, h * 64:(h + 1) * 64], TTm[:, slot_(h), :], Xs[:, h, :], True, True, [TK, "Xs"], [PK(bU)])
                    CP("act", Us[:].rearrange("p h t -> p (h t)"), ps[bU][0:64, :], [PK(bU)], ["Us"])
                    yield
                    bH = sbank()
                    for h in range(8):
                        c, pr = hp(h)
                        o = ps[bH][pr, c * 64:(c + 1) * 64]
                        MM(o, Btok[:, c, pr], Us[:, h, :], True, False, [BK, "Us"], [PK(bH)])
                        MM(o, Ktok[:, c, pr], vtokb[:, j, h * 64:(h + 1) * 64], False, True, [KK, VTB], [PK(bH)])
                    bY = sbank()
                    for c in range(4):
                        o = ps[bY][0:64, c * 128:(c + 1) * 128]
                        MM(o, AR[c][:, j, 1, :], Hb[:, c, :], True, False, [f"AR{c}" + PS, "Hb"], [PK(bY)])
                        for i in range(2):
                            h = 2 * c + i
                            oh = ps[bY][0:64, h * 64:(h + 1) * 64]
                            MM(oh, A1s[:, slot_(h), 64:128], Us[:, h, :], False, False, [A1K, "Us"], [PK(bY)])
                            MM(oh, A2s[:, slot_(h), 64:128], vtokb[:, j, h * 64:(h + 1) * 64], False, i == 1, [A2K, VTB], [PK(bY)])
                    for c in range(4):
                        for i in range(2):
                            pr = slice(64 * i, 64 * i + 64)
                            STT(Hblk[pr, c, i * 64:(i + 1) * 64], Hblk[pr, c, i * 64:(i + 1) * 64], eLx[c][pr, j, 64:65],
                                ps[bH][pr, c * 64:(c + 1) * 64], ALU.mult, ALU.add, ["Hst", f"eLx{c}" + PS, PK(bH)], ["Hst"])
                    CP("dve", Hb[:].rearrange("p c v -> p (c v)"), Hblk[:].rearrange("p c v -> p (c v)"), ["Hst"], ["Hb"])
                    CP("act", ysb[:], ps[bY][0:64, :], [PK(bY)], [YK])
                    if dbg:
                        DMA("sp", dbg_d["yscan"][n, j], ysb[:], [YK], (), "dbg_ys")
                    yield
                for j in range(2):
                    js = slice(j * 64, (j + 1) * 64)
                    VT = f"vtok{j}" + VK
                    ysb = ysb2[j]
                    YK = f"ysb{j}"
                    y3 = ysb[:].rearrange("p (h d) -> p h d", d=64)
                    S.op("dve", lambda e, y3=y3: e.tensor_reduce(out=s1[:], in_=y3, axis=AX.X, op=ALU.add), [YK], ["s1"])
                    ACTF(ysq[:], ysb[:], ACT.Square, [YK], ["ysq"])
                    S.op("dve", lambda e: e.tensor_reduce(out=s2[:], in_=ysq[:].rearrange("p (h d) -> p h d", d=64), axis=AX.X, op=ALU.add),
                         ["ysq"], ["s2"])
                    TS("dve", mean[:], s1[:], 1.0 / 64, None, ALU.mult, None, ["s1"], ["mean"])
                    TT("dve", msq[:], mean[:], mean[:], ALU.mult, ["mean"], ["msq"])
                    STT(var[:], s2[:], 1.0 / 64, msq[:], ALU.mult, ALU.subtract, ["s2", "msq"], ["var"])
                    TS("dve", var[:], var[:], GN_EPS, None, ALU.add, None, ["var"], ["var"])
                    TT("pool", rstd[:], var[:], cm05[0:64, 0:8], ALU.pow, ["var", "cm05"], ["rstd"])
                    yield
                    yc3 = yc[:].rearrange("p (h d) -> p h d", d=64)
                    TT("dve", yc3, y3, mean[:].unsqueeze(2).to_broadcast([64, 8, 64]), ALU.subtract, [YK, "mean"], ["yc"])
                    TT("dve", yc3, yc3, rstd[:].unsqueeze(2).to_broadcast([64, 8, 64]), ALU.mult, ["yc", "rstd"], ["yc"])
                    TT("dve", yc[:], yc[:], lng_bc[:], ALU.mult, ["yc", "lng_bc"], ["yc"])
                    TT("pool", bv[:].rearrange("p (h d) -> p h d", d=64), vtok[:, j, :].rearrange("p (h d) -> p h d", d=64),
                       bon[:, j, :].unsqueeze(2).to_broadcast([64, 8, 64]), ALU.mult, [VT, "bon" + PS], ["bv"])
                    TT("pool", bv[:], bv[:], lnb_bc[:], ALU.add, ["bv", "lnb_bc"], ["bv"])
                    TT("pool", yc[:], yc[:], bv[:], ALU.add, ["yc", "bv"], ["yc"])
                    yield
                    bg_ = sbank()
                    MM(ps[bg_][0:64, :], sgT[:, js], gup[:], True, True, ["sgT" + VK, "gup"], [PK(bg_)])
                    TT("dve", ya_tok[:], yc[:], ps[bg_][0:64, :], ALU.mult, ["yc", PK(bg_)], ["ya_tok"])
                    if dbg:
                        DMA("sp", dbg_d["ya_tok"][n, j], ya_tok[:], ["ya_tok"], (), "dbg_ya")
                    bt_ = sbank()
                    for c in range(4):
                        TR(ps[bt_][:, c * 64:(c + 1) * 64], ya_tok[:, c * 128:(c + 1) * 128], ["ya_tok"], [PK(bt_)], k=64)
                    CP("act", yaT[:, :, js], ps[bt_][:, 0:256].rearrange("p (c t) -> p c t", t=64), [PK(bt_)], ["yaT"])
                    yield
                DMA("sp", ysp_d[0, :, n], yaT[:], ["yaT"], (), "st_yaT")
                yield

            def run_all(g):
                for _ in g:
                    pass

            def interleave(g1, g2, r1=2, r2=1):
                a1 = a2 = True
                while a1 or a2:
                    for _ in range(r1):
                        if a1:
                            try:
                                next(g1)
                            except StopIteration:
                                a1 = False
                    for _ in range(r2):
                        if a2:
                            try:
                                next(g2)
                            except StopIteration:
                                a2 = False

            if _os.environ.get("K_SBUF"):
                live = {k: v for k, v in sb_bytes.items()}
                print("A1 SBUF bytes/partition (incl. persistent + freed setup temps):", sum(live.values()))
                print(sorted(live.items(), key=lambda kv: -kv[1])[:40])
            IR1 = int(_os.environ.get("K_IR1", "1"))
            IR2 = int(_os.environ.get("K_IR2", "1"))

            def chain(*gs):
                for g in gs:
                    yield from g

            run_all(chain(front(0), stage8(0)))
            for n in range(NB):
                if n + 1 < NB:
                    interleave(scan_post(n), chain(front(n + 1), stage8(n + 1)), IR1, IR2)
                else:
                    run_all(scan_post(n))
            S.barrier()
            S.emit()

        if stop_after == "A1":
            return nc

        def run_all(g):
            for _ in g:
                pass

        def interleave(g1, g2, r1=1, r2=1):
            a1 = a2 = True
            while a1 or a2:
                for _ in range(r1):
                    if a1:
                        try:
                            next(g1)
                        except StopIteration:
                            a1 = False
                for _ in range(r2):
                    if a2:
                        try:
                            next(g2)
                        except StopIteration:
                            a2 = False

        PB = [0, 1]
        MB = [2, 3, 4, 5, 6, 7]
        pctr = [0]
        mctr = [0]

        def pbank():
            b = PB[pctr[0] % len(PB)]
            pctr[0] += 1
            return b

        def mbank():
            b = MB[mctr[0] % len(MB)]
            mctr[0] += 1
            return b

        def norm_gen(e_x, hdst, gs, sh_col0, keyx, keyh, s, xn_t, ss_t, tag):
            XN, SS = "xn" + tag, "ss" + tag
            ACTF(xn_t[:], e_x, ACT.Square, [keyx], [XN, SS], accum=ss_t[:, 0:1])
            TS("dve", ss_t[:, 1:2], ss_t[:, 0:1], 1.0 / D, RMS_EPS, ALU.mult, ALU.add, [SS], [SS + "1"])
            rsqrt_pool(ss_t[:, 2:3], ss_t[:, 1:2], [SS + "1"], [SS + "2"])
            ACTF(xn_t[:], e_x, ACT.Identity, [keyx, SS + "2", SS], [XN], scale=ss_t[:, 2:3])
            yield
            for half in range(2):
                b = pbank()
                for q in range(4):
                    c = half * 4 + q
                    TR(ps[b][:, q * 128:(q + 1) * 128], xn_t[:, c * 128:(c + 1) * 128], [XN], [PK(b)])
                for q in range(4):
                    c = half * 4 + q
                    o = hdst[:, c, s * 128:(s + 1) * 128]
                    if q % 2 == 0:
                        ACTF(o, ps[b][:, q * 128:(q + 1) * 128], ACT.Identity, [PK(b), "adaT"], [keyh],
                             scale=gs[:, c:c + 1], bias=adaT[:, sh_col0 + c:sh_col0 + c + 1])
                    else:
                        TS("dve", o, ps[b][:, q * 128:(q + 1) * 128], gs[:, c:c + 1], adaT[:, sh_col0 + c:sh_col0 + c + 1],
                           ALU.mult, ALU.add, [PK(b), "adaT"], [keyh])
                yield

        g1h_bc = sb(es, "g1h_bc", [128, D])
        g2_bc = sb(es, "g2_bc", [128, D])
        with ExitStack() as eg:
            diag = sb(eg, "diag", [128, 128])
            for gi, (col0, scl, dst, key) in enumerate(((16, 0.5, g1h_bc, "g1h_bc"), (40, 1.0, g2_bc, "g2_bc"))):
                for half in range(2):
                    bG = bank()
                    for q in range(4):
                        m = half * 4 + q
                        TS("dve", diag[:], ident[:], adaT[:, col0 + m:col0 + m + 1], scl, ALU.mult, ALU.mult, ["ident", "adaT"], ["diag"])
                        MM(ps[bG][:, q * 128:(q + 1) * 128], ones_f[:], diag[:], True, True, ["ones_f", "diag"], [PK(bG)])
                    CP("act", dst[:, half * 512:(half + 1) * 512], ps[bG][:, :], [PK(bG)], [key])
            S.barrier()
            S.emit()

        TB2 = min(512, S_len)
        NT2 = S_len // TB2
        SB2 = TB2 // 128
        with ExitStack() as e2:
            wg = sb(e2, "wg", [128, 8, 2048], BF16)
            wba = sb(e2, "wba", [128, 4, D], BF16)
            wbb = sb(e2, "wbb", [128, 4, D], BF16)
            wo = sb(e2, "wo", [128, 8, D], BF16)
            stg2 = [sb(e2, f"stg2_{i}", [128, D]) for i in range(2)]
            NG2 = 4
            for g_ in range(NG2):
                ca = slice(g_ * 256, (g_ + 1) * 256)
                cb_ = slice(1024 + g_ * 256, 1024 + (g_ + 1) * 256)
                for c in range(8):
                    DMA("pool", wg[:, c, ca], w_in_d[c * 128:(c + 1) * 128, 2560 + g_ * 256:2560 + (g_ + 1) * 256], (), [f"wg_{g_}_{c}a"], f"ld_wg_{g_}")
                    DMA("pool", wg[:, c, cb_], w_in_d[c * 128:(c + 1) * 128, 3584 + g_ * 256:3584 + (g_ + 1) * 256], (), [f"wg_{g_}_{c}b"], f"ld_wg_{g_}")
                for c in range(4):
                    DMA("pool", wba[:, c, ca], wba_d[c * 128:(c + 1) * 128, ca], (), [f"wba_{g_}_{c}"], f"ld_wg_{g_}")
                    DMA("pool", wbb[:, c, ca], wbb_d[c * 128:(c + 1) * 128, ca], (), [f"wbb_{g_}_{c}"], f"ld_wg_{g_}")

            def wgk(m):
                g_ = m // 2
                return [f"wg_{g_}_{c}{x}" for c in range(8) for x in "ab"] + [f"wba_{g_}_{c}" for c in range(4)] + [f"wbb_{g_}_{c}" for c in range(4)]

            def a2_wo():
                for c in range(8):
                    sp_ = c % 2
                    DMA("sp", stg2[sp_][:], wout_d[c * 128:(c + 1) * 128, :], (), [f"stg2_{sp_}"], f"ld_stg2_{sp_}")
                    TT("dve", wo[:, c, :], stg2[sp_][:], g1h_bc[:], ALU.mult, [f"stg2_{sp_}", "g1h_bc"], ["wo"])
                    yield
            xt2 = [sb(e2, f"xt_{i}", [128, SB2, D]) for i in range(2)]
            xn2 = sb(e2, "xn2", [128, D])
            ssb = sb(e2, "ssb", [128, 4])
            h22 = [sb(e2, f"h2_{i}", [128, 8, TB2], BF16) for i in range(2)]
            ya22 = [sb(e2, f"ya2_{i}", [128, 4, SB2, 128], BF16) for i in range(2)]
            yb22 = [sb(e2, f"yb2_{i}", [128, 4, SB2, 128], BF16) for i in range(2)]
            gta = sb(e2, "gta", [128, TB2], BF16)
            gtb = sb(e2, "gtb", [128, TB2], BF16)
            ua = sb(e2, "ua", [128, TB2])
            ub = sb(e2, "ub", [128, TB2])
            m2T = sb(e2, "m2T", [128, 8, TB2], BF16)
            x1t = [sb(e2, f"x1t{i}", [128, D]) for i in range(2)]

            def a2_prep(t):
                par = t % 2
                xt, h2, ya2, yb2 = xt2[par], h22[par], ya22[par], yb22[par]
                for s_ in range(SB2):
                    DMA("sp", xt[:, s_, :], x_d[t * TB2 + s_ * 128:t * TB2 + (s_ + 1) * 128, :], (), [f"xt{s_}_{par}"], f"ld_xt{s_}_{par}")
                for s_ in range(SB2):
                    DMA("pool", ya2[:, :, s_, :], ysp_d[0, :, t * SB2 + s_], (), [f"ya2_{s_}_{par}"], f"ld_ya2_{par}")
                    DMA("pool", yb2[:, :, s_, :], ysp_d[1, :, t * SB2 + s_], (), [f"yb2_{s_}_{par}"], f"ld_yb2_{par}")
                yield
                for s_ in range(SB2):
                    yield from norm_gen(xt[:, s_, :], h2, gs1, 0, f"xt{s_}_{par}", f"h2_{par}", s_, xn2, ssb, "A2")

            def a2_main(t):
                par = t % 2
                xt, h2, ya2, yb2 = xt2[par], h22[par], ya22[par], yb22[par]
                H2K = f"h2_{par}"
                YA = [f"ya2_{s_}_{par}" for s_ in range(SB2)]
                YB = [f"yb2_{s_}_{par}" for s_ in range(SB2)]
                for m in range(8):
                    bga, bgb_ = mbank(), mbank()
                    for c in range(8):
                        MM(ps[bga][:, 0:TB2], wg[:, c, m * 128:(m + 1) * 128], h2[:, c, :], c == 0, c == 7, wgk(m) + [H2K], [PK(bga)])
                    for c in range(8):
                        MM(ps[bgb_][:, 0:TB2], wg[:, c, 1024 + m * 128:1024 + (m + 1) * 128], h2[:, c, :], c == 0, c == 7, wgk(m) + [H2K], [PK(bgb_)])
                    ACTF(gta[:], ps[bga][:, 0:TB2], ACT.Tanh, [PK(bga), "hbg"], ["gta"], scale=0.5, bias=hbg[:, m:m + 1])
                    ACTF(gtb[:], ps[bgb_][:, 0:TB2], ACT.Tanh, [PK(bgb_), "hbg"], ["gtb"], scale=0.5, bias=hbg[:, 8 + m:9 + m])
                    bpa, bpb = mbank(), mbank()
                    for c in range(4):
                        MM(ps[bpa][:, 0:TB2], wba[:, c, m * 128:(m + 1) * 128], ya2[:, c, :, :], c == 0, c == 3, wgk(m) + YA, [PK(bpa)])
                    for c in range(4):
                        MM(ps[bpb][:, 0:TB2], wbb[:, c, m * 128:(m + 1) * 128], yb2[:, c, :, :], c == 0, c == 3, wgk(m) + YB, [PK(bpb)])
                    STT(ua[:], gta[:], 1.0, ps[bpa][:, 0:TB2], ALU.add, ALU.mult, ["gta", PK(bpa)], ["ua"])
                    STT(ub[:], gtb[:], 1.0, ps[bpb][:, 0:TB2], ALU.add, ALU.mult, ["gtb", PK(bpb)], ["ub"])
                    TT("pool", m2T[:, m, :], ua[:], ub[:], ALU.add, ["ua", "ub"], ["m2T"])
                    yield
                for s_ in range(SB2):
                    xp = s_ % 2
                    for half in range(2):
                        bo = mbank()
                        for m in range(8):
                            MM(ps[bo][:, :], m2T[:, m, s_ * 128:(s_ + 1) * 128], wo[:, m, half * 512:(half + 1) * 512], m == 0, m == 7,
                               ["m2T", "wo"], [PK(bo)])
                        TT("dve", x1t[xp][:, half * 512:(half + 1) * 512], ps[bo][:, :], xt[:, s_, half * 512:(half + 1) * 512], ALU.add,
                           [PK(bo), f"xt{s_}_{par}"], [f"x1t{xp}"])
                        yield
                    DMA("sp", out_d[t * TB2 + s_ * 128:t * TB2 + (s_ + 1) * 128, :], x1t[xp][:], [f"x1t{xp}"], (), f"st_x1t{xp}")

            run_all(a2_prep(0))

            def chain3(*gs):
                for g in gs:
                    yield from g

            for t in range(NT2):
                if t == 0 and NT2 > 1:
                    interleave(a2_main(0), chain3(a2_wo(), a2_prep(1)), 1, 1)
                elif t == 0:
                    run_all(a2_wo())
                    run_all(a2_main(0))
                elif t + 1 < NT2:
                    interleave(a2_main(t), a2_prep(t + 1), 1, 1)
                else:
                    run_all(a2_main(t))
            S.barrier()
            S.emit()

        if stop_after == "A2":
            return nc

        TB3 = min(256, S_len)
        NT3 = S_len // TB3
        SB3 = TB3 // 128
        with ExitStack() as e3:
            w1 = sb(e3, "w1", [128, 8, DFF], BF16)
            w3 = sb(e3, "w3", [128, 8, DFF], BF16)
            w2 = sb(e3, "w2", [128, NFF, D], BF16)
            xt3 = [sb(e3, f"xtb_{i}", [128, SB3, D]) for i in range(2)]
            xn3 = sb(e3, "xn2b", [128, D])
            ssb3 = sb(e3, "ssbb", [128, 4])
            h23 = [sb(e3, f"h2b_{i}", [128, 8, TB3], BF16) for i in range(2)]
            sg = [sb(e3, f"sg{i}", [128, TB3]) for i in range(2)]
            aT = sb(e3, "aT", [128, NFF, TB3], BF16)
            stg3 = [sb(e3, f"stg3_{i}", [128, D]) for i in range(2)]
            NCB = 4
            CBW = DFF // NCB
            def b_wload(cb):
                cs = slice(cb * CBW, (cb + 1) * CBW)
                for c in range(8):
                    DMA("pool", w1[:, c, cs], w1_d[c * 128:(c + 1) * 128, cs], (), [f"w1_{cb}_{c}"], f"ld_w1_{cb}")
                    DMA("pool", w3[:, c, cs], w3_d[c * 128:(c + 1) * 128, cs], (), [f"w3_{cb}_{c}"], f"ld_w3_{cb}")

            def w_keys(nm, f):
                lo, hi = f * 128, (f + 1) * 128 - 1
                return [f"{nm}_{cb}_{c}" for cb in sorted({lo // CBW, hi // CBW}) for c in range(8)]

            def b_w2():
                for f in range(NFF):
                    sp_ = f % 2
                    DMA("sp", stg3[sp_][:], w2_d[f * 128:(f + 1) * 128, :], (), [f"stg3_{sp_}"], f"ld_stg3_{sp_}")
                    TT("dve", w2[:, f, :], stg3[sp_][:], g2_bc[:], ALU.mult, [f"stg3_{sp_}", "g2_bc"], ["w2"])
                    yield

            def b_prep(t):
                par = t % 2
                xt, h2 = xt3[par], h23[par]
                for s_ in range(SB3):
                    DMA("sp", xt[:, s_, :], out_d[t * TB3 + s_ * 128:t * TB3 + (s_ + 1) * 128, :], (), [f"xtb{s_}_{par}"], f"ld_xtb{s_}_{par}")
                yield
                for s_ in range(SB3):
                    yield from norm_gen(xt[:, s_, :], h2, gs2, 24, f"xtb{s_}_{par}", f"h2b_{par}", s_, xn3, ssb3, "B")

            def b_main(t):
                par = t % 2
                xt, h2 = xt3[par], h23[par]
                H2K = f"h2b_{par}"
                for f in range(NFF):
                    bg1, bu1 = mbank(), mbank()
                    for c in range(8):
                        MM(ps[bg1][:, 0:TB3], w1[:, c, f * 128:(f + 1) * 128], h2[:, c, :], c == 0, c == 7, w_keys("w1", f) + [H2K], [PK(bg1)])
                    for c in range(8):
                        MM(ps[bu1][:, 0:TB3], w3[:, c, f * 128:(f + 1) * 128], h2[:, c, :], c == 0, c == 7, w_keys("w3", f) + [H2K], [PK(bu1)])
                    ACTF(sg[f % 2][:], ps[bg1][:, 0:TB3], ACT.Silu, [PK(bg1)], [f"sg{f % 2}"])
                    TT("dve", aT[:, f, :], sg[f % 2][:], ps[bu1][:, 0:TB3], ALU.mult, [f"sg{f % 2}", PK(bu1)], ["aT"])
                    if f % 2 == 1:
                        yield
                for s_ in range(SB3):
                    for half in range(2):
                        bo = mbank()
                        for f in range(NFF):
                            MM(ps[bo][:, :], aT[:, f, s_ * 128:(s_ + 1) * 128], w2[:, f, half * 512:(half + 1) * 512], f == 0, f == NFF - 1,
                               ["aT", "w2"], [PK(bo)])
                        TT("dve", xt[:, s_, half * 512:(half + 1) * 512], ps[bo][:, :], xt[:, s_, half * 512:(half + 1) * 512], ALU.add,
                           [PK(bo), f"xtb{s_}_{par}"], [f"xtb{s_}_{par}"])
                        yield
                    DMA("sp", out_d[t * TB3 + s_ * 128:t * TB3 + (s_ + 1) * 128, :], xt[:, s_, :], [f"xtb{s_}_{par}"], (), f"st_ot{s_}_{par}")

            b_wload(0)
            run_all(b_prep(0))
            for cb in range(1, NCB):
                b_wload(cb)

            def chain2(*gs):
                for g in gs:
                    yield from g

            for t in range(NT3):
                if t == 0 and NT3 > 1:
                    interleave(b_main(0), chain2(b_w2(), b_prep(1)), 1, 2)
                elif t == 0:
                    run_all(b_w2())
                    run_all(b_main(0))
                elif t + 1 < NT3:
                    interleave(b_main(t), b_prep(t + 1), 2, 1)
                else:
                    run_all(b_main(t))
            S.barrier()
            S.emit()
    return nc


def _consts():
    j = np.arange(64)[:, None]
    t = np.arange(64)[None, :]
    su = (j < t).astype(np.float32)
    ui = (j <= t).astype(np.float32)
    mask1 = np.concatenate([su, ui], axis=1)
    masksl = (t < j).astype(np.float32)
    kk = np.arange(128)[:, None]
    qq = np.arange(128)[None, :]
    amask = np.stack([(kk > qq), (kk <= qq)], axis=1).astype(np.float32)
    bones = np.zeros((128, 128), np.float32)
    bones[:64, :64] = 1.0
    bones[64:, 64:] = 1.0
    hind = np.zeros((128, 2), np.float32)
    hind[:64, 0] = 1.0
    hind[64:, 1] = 1.0
    half = 8
    invf = (np.float32(500000.0) ** (-np.arange(half, dtype=np.float32) / np.float32(half))).astype(np.float32)[None, :]
    return dict(ident=np.eye(128, dtype=np.float32), mask1=mask1, masksl=masksl, amask=amask, bones=bones, hind=hind, invf=invf)


def _pp(v, k):
    return np.ascontiguousarray(np.asarray(v, np.float32).reshape(k, 128).T)


def make_in_maps(inputs, S_len=4096, cores=None):
    f = lambda a: np.ascontiguousarray(np.asarray(a, np.float32))
    NB = S_len // 128
    shared = dict(
        ada_w=f(inputs["ada_w"][0]), ada_bT=_pp(inputs["ada_b"][0], 48),
        n1g=_pp(inputs["norm1_gain"][0], 8), n2g=_pp(inputs["norm2_gain"][0], 8),
        w_in=f(inputs["w_in"][0]), mu=f(inputs["tshift_mu"][0])[None, :],
        w0T=_pp(inputs["decay_w0"][0], 4), a0T=_pp(inputs["iclr_a0"][0], 4),
        kkT=_pp(inputs["k_k"][0], 4), kaT=_pp(inputs["k_a"][0], 4), rkT=_pp(np.asarray(inputs["r_k"][0]).reshape(-1), 4),
        lora_up=f(np.concatenate([inputs["decay_up"][0], inputs["iclr_up"][0]], axis=0)),
        gate_up=f(inputs["gate_up"][0]),
        lnx_g=f(inputs["lnx_gain"][0])[None, :], lnx_b=f(inputs["lnx_bias"][0])[None, :],
        qkg=f(np.concatenate([np.tile(np.asarray(inputs["q_norm_gain"][0]), 8), np.tile(np.asarray(inputs["k_norm_gain"][0]), 2)]))[None, :],
        sinks=f(inputs["attn_sinks"][0])[None, :],
        bgbT=_pp(inputs["branch_gate_b"][0], 16),
        wba=f(inputs["w_branch_a"][0]), wbb=f(inputs["w_branch_b"][0]), w_out=f(inputs["w_out"][0]),
        w1=f(inputs["ffn_w1"][0]), w3=f(inputs["ffn_w3"][0]), w2=f(inputs["ffn_w2"][0]),
    )
    shared.update(_consts())
    maps = []
    cores = range(8) if cores is None else cores
    for b in cores:
        m = dict(shared)
        m["x"] = f(inputs["x"][b, :S_len])
        cT = np.zeros((128, 8, 2), np.float32)
        cT[:, :, 0] = _pp(inputs["c"][b], 8)
        m["cT"] = cT
        m["pos"] = np.ascontiguousarray(np.asarray(inputs["positions"][b, :S_len], np.int32).reshape(NB, 128).T)
        maps.append(m)
    return maps


_NC_CACHE = {}


def kernel(**inputs):
    if "nc" not in _NC_CACHE:
        _NC_CACHE["nc"] = build(4096, "B", False)
    nc = _NC_CACHE["nc"]
    maps = make_in_maps(inputs, 4096)
    res = run_bass_kernel_spmd(nc, maps, core_ids=list(range(8)))
    out = np.stack([np.asarray(r["out"], np.float32) for r in res.results], axis=0)
    return out
```

```python
import numpy as np
from contextlib import ExitStack
import concourse.bass as bass
import concourse.mybir as mybir
from concourse.bass_utils import run_bass_kernel_spmd

F32 = mybir.dt.float32
BF16 = mybir.dt.bfloat16
I32 = mybir.dt.int32
ACT = mybir.ActivationFunctionType
ALU = mybir.AluOpType
AX = mybir.AxisListType

D = 1024
DFF = 2816
NFF = DFF // 128
RMS_EPS = 1e-6
GN_EPS = 64e-5
KDEC = float(0.5 * np.exp(-0.5))
SDT = BF16


class Sched:
    ENG = ("pe", "dve", "act", "pool", "sp")

    def __init__(self, nc, es):
        self.nc = nc
        self.es = es
        self.thunks = {e: [] for e in self.ENG}
        self.sems = {}
        self.count = {}
        self.waited = {e: {} for e in self.ENG}
        self.res = {}
        for e in self.ENG:
            self._mk_sem("e_" + e)
        self.n_ins = 0
        self.n_wait = 0

    def _mk_sem(self, name):
        self.sems[name] = self.es.enter_context(self.nc.semaphore(name))
        self.count[name] = 0

    def _wait(self, eng, s, v):
        W = self.waited[eng]
        if W.get(s, 0) < v:
            W[s] = v
            sem = self.sems[s]
            self.thunks[eng].append(lambda e, sem=sem, v=v: e.wait_ge(sem, v))
            self.n_wait += 1

    def op(self, eng, fn, reads=(), writes=(), chan=None):
        own = "e_" + eng
        need = {}
        for k in reads:
            r = self.res.get(k)
            if r and r["w"]:
                s, v = r["w"]
                need[s] = max(need.get(s, 0), v)
        for k in writes:
            r = self.res.get(k)
            if r:
                if r["w"]:
                    s, v = r["w"]
                    need[s] = max(need.get(s, 0), v)
                for (s, v) in r["r"]:
                    need[s] = max(need.get(s, 0), v)
        for s, v in need.items():
            if s == own and eng == "pe" and chan is None:
                continue
            if not s.startswith("e_"):
                v = self.count[s]
            self._wait(eng, s, v)
        if chan is None:
            s = own
            inc = 1
        else:
            s = chan
            if s not in self.sems:
                self._mk_sem(s)
            inc = 16
        self.count[s] += inc
        tag = (s, self.count[s])
        sem = self.sems[s]
        self.thunks[eng].append(lambda e, fn=fn, sem=sem, inc=inc: fn(e).then_inc(sem, inc))
        self.n_ins += 1
        for k in reads:
            r = self.res.setdefault(k, {"w": None, "r": []})
            r["r"].append(tag)
            if len(r["r"]) > 64:
                best = {}
                for (s2, v2) in r["r"]:
                    best[s2] = max(best.get(s2, 0), v2)
                r["r"] = list(best.items())
        for k in writes:
            self.res[k] = {"w": tag, "r": []}
        return tag

    def barrier(self):
        for eng in self.ENG:
            for s, v in self.count.items():
                if v > 0 and not (s == "e_" + eng):
                    self._wait(eng, s, v)

    def emit(self):
        nc = self.nc
        th = self.thunks
        self.thunks = {e: [] for e in self.ENG}
        with nc.Block() as block:
            @block.tensor
            def _(e):
                for t in th["pe"]:
                    t(e)

            @block.vector
            def _(e):
                for t in th["dve"]:
                    t(e)

            @block.scalar
            def _(e):
                for t in th["act"]:
                    t(e)

            @block.gpsimd
            def _(e):
                for t in th["pool"]:
                    t(e)

            @block.sync
            def _(e):
                for t in th["sp"]:
                    t(e)


def build(S_len=4096, stop_after="B", dbg=False):
    NB = S_len // 128
    nc = bass.Bass("TRN2", target_bir_lowering=False)

    def din(name, shape, dt=F32):
        return nc.dram_tensor(name, list(shape), dt, kind="ExternalInput").ap()

    x_d = din("x", [S_len, D])
    cT_d = din("cT", [128, 8, 2])
    pos_d = din("pos", [128, NB], I32)
    ada_w_d = din("ada_w", [D, 6 * D])
    ada_b_d = din("ada_bT", [128, 48])
    n1g_d = din("n1g", [128, 8])
    n2g_d = din("n2g", [128, 8])
    w_in_d = din("w_in", [D, 4608])
    mu_d = din("mu", [1, 1792])
    w0_d = din("w0T", [128, 4])
    a0_d = din("a0T", [128, 4])
    kk_d = din("kkT", [128, 4])
    ka_d = din("kaT", [128, 4])
    rk_d = din("rkT", [128, 4])
    lora_d = din("lora_up", [128, 512])
    gup_d = din("gate_up", [128, 512])
    lng_d = din("lnx_g", [1, 512])
    lnb_d = din("lnx_b", [1, 512])
    qkg_d = din("qkg", [1, 640])
    sink_d = din("sinks", [1, 8])
    bgb_d = din("bgbT", [128, 16])
    wba_d = din("wba", [512, D])
    wbb_d = din("wbb", [512, D])
    wout_d = din("w_out", [D, D])
    w1_d = din("w1", [D, DFF])
    w3_d = din("w3", [D, DFF])
    w2_d = din("w2", [DFF, D])
    ident_d = din("ident", [128, 128])
    mask1_d = din("mask1", [64, 128])
    masksl_d = din("masksl", [64, 64])
    amask_d = din("amask", [128, 2, 128])
    bones_d = din("bones", [128, 128])
    hind_d = din("hind", [128, 2])
    invf_d = din("invf", [1, 8])

    out_d = nc.dram_tensor("out", [S_len, D], F32, kind="ExternalOutput").ap()
    ysp_kind = "ExternalOutput" if dbg else "Internal"
    ysp_d = nc.dram_tensor("ysp", [2, 128, NB, 4, 128], BF16, kind=ysp_kind).ap()
    dbg_d = {}
    if dbg:
        dbg_d["adaT"] = nc.dram_tensor("dbg_adaT", [128, 48], F32, kind="ExternalOutput").ap()
        dbg_d["ya_tok"] = nc.dram_tensor("dbg_ya_tok", [NB, 2, 64, 512], F32, kind="ExternalOutput").ap()
        dbg_d["yscan"] = nc.dram_tensor("dbg_yscan", [NB, 2, 64, 512], F32, kind="ExternalOutput").ap()
        dbg_d["yb_tok"] = nc.dram_tensor("dbg_yb_tok", [NB, 128, 512], F32, kind="ExternalOutput").ap()

    with ExitStack() as es:
        S = Sched(nc, es)

        def TT(eng, out, in0, in1, op, r, w):
            S.op(eng, lambda e: e.tensor_tensor(out=out, in0=in0, in1=in1, op=op), r, w)

        def TS(eng, out, in0, s1, s2, op0, op1, r, w):
            if s2 is None:
                S.op(eng, lambda e: e.tensor_scalar(out=out, in0=in0, scalar1=s1, scalar2=None, op0=op0), r, w)
            else:
                S.op(eng, lambda e: e.tensor_scalar(out=out, in0=in0, scalar1=s1, scalar2=s2, op0=op0, op1=op1), r, w)

        def STT(out, in0, scalar, in1, op0, op1, r, w):
            S.op("dve", lambda e: e.scalar_tensor_tensor(out=out, in0=in0, scalar=scalar, in1=in1, op0=op0, op1=op1), r, w)

        def ACTF(out, in_, func, r, w, scale=1.0, bias=0.0, accum=None):
            if accum is None:
                S.op("act", lambda e: e.activation(out=out, in_=in_, func=func, bias=bias, scale=scale), r, w)
            else:
                S.op("act", lambda e: e.activation(out=out, in_=in_, func=func, bias=bias, scale=scale, accum_out=accum), r, w)

        def CP(eng, out, in_, r, w):
            if eng == "act":
                S.op("act", lambda e: e.activation(out=out, in_=in_, func=ACT.Copy), r, w)
            else:
                S.op(eng, lambda e: e.tensor_copy(out=out, in_=in_), r, w)

        def MM(out, lhsT, rhs, start, stop, r, w):
            S.op("pe", lambda e: e.matmul(out, lhsT=lhsT, rhs=rhs, start=start, stop=stop), r, w)

        def DMA(eng, out, in_, r, w, chan, **kw):
            if chan == "bulk":
                chan = "bulk_" + eng
            S.op(eng, lambda e: e.dma_start(out=out, in_=in_, **kw), r, w, chan=chan)

        def MEMSET(eng, ap, val, w):
            S.op(eng, lambda e: e.memset(ap, val), (), w)

        ps = [es.enter_context(nc.psum_tensor(f"ps{i}", [128, 512], F32)) for i in range(8)]
        bank_ctr = [0]

        def bank():
            b = bank_ctr[0] % 8
            bank_ctr[0] += 1
            return b

        FRONT_BANKS = [0, 1, 2]
        SCAN_BANKS = [3, 4, 5, 6, 7]
        fctr = [0]
        sctr = [0]

        def fbank():
            b = FRONT_BANKS[fctr[0] % len(FRONT_BANKS)]
            fctr[0] += 1
            return b

        def sbank():
            b = SCAN_BANKS[sctr[0] % len(SCAN_BANKS)]
            sctr[0] += 1
            return b

        def PK(b):
            return f"ps{b}"

        import os as _os
        KSTOP = _os.environ.get("K_STOP", "")

        class _Stop(Exception):
            pass

        def stop_here(tag):
            if KSTOP == tag:
                S.barrier()
                S.emit()
                raise _Stop()

        sb_bytes = {}

        def sb(stack, name, shape, dt=F32):
            n_ = 1
            for d_ in shape[1:]:
                n_ *= d_
            sb_bytes[name] = n_ * (2 if dt == BF16 else 4)
            return stack.enter_context(nc.sbuf_tensor("s_" + name, list(shape), dt))

        ident = sb(es, "ident", [128, 128])
        ones_f = sb(es, "ones_f", [128, 128])
        cm05 = sb(es, "cm05", [128, 128])
        adaT = sb(es, "adaT", [128, 48])
        gs1 = sb(es, "gs1", [128, 8])
        gs2 = sb(es, "gs2", [128, 8])
        hbg = sb(es, "hbg", [128, 16])

        def TR(out, in_, r, w, k=128):
            S.op("pe", lambda e: e.transpose(out=out, in_=in_, identity=ident[0:k, 0:k]), list(r) + ["ident"], w)

        DMA("sp", ident[:], ident_d, (), ["ident"], "c_ident")
        MEMSET("pool", ones_f[:], 1.0, ["ones_f"])
        MEMSET("pool", cm05[:], -0.5, ["cm05"])

        def rsqrt_pool(out, in_, r, w, nparts=128):
            shp = list(in_.shape)
            cv = cm05[0:nparts, 0:1].to_broadcast(shp) if len(shp) == 2 else None
            TT("pool", out, in_, cv, ALU.pow, list(r) + ["cm05"], w)

        with ExitStack() as e1:
            wA = sb(e1, "wA", [128, 8, 2560], BF16)
            wS = sb(e1, "wS", [128, 8, 1792], BF16)
            lora_up = sb(e1, "lora_up", [128, 512])
            gup = sb(e1, "gup", [128, 512], BF16)
            mask1 = sb(e1, "mask1", [64, 128])
            masksl = sb(e1, "masksl", [64, 64])
            amask = sb(e1, "amask", [128, 2, 128], BF16)
            bones = sb(e1, "bones", [128, 128])
            hind = sb(e1, "hind", [128, 2])
            lng_bc = sb(e1, "lng_bc", [64, 512])
            lnb_bc = sb(e1, "lnb_bc", [64, 512])
            qkg_bc = sb(e1, "qkg_bc", [128, 640])
            esink = sb(e1, "esink", [128, 8])
            cos_t = sb(e1, "cos_t", [128, NB, 8])
            sin_t = sb(e1, "sin_t", [128, NB, 8])
            hw0 = sb(e1, "hw0", [128, 4])
            ha0 = sb(e1, "ha0", [128, 4])
            kkT = sb(e1, "kkT", [128, 4])
            hka = sb(e1, "hka", [128, 4])
            omhka = sb(e1, "omhka", [128, 4])
            rkT = sb(e1, "rkT", [128, 4])
            ones64 = sb(e1, "ones64", [128, 64])
            negk = sb(e1, "negk", [128, 1])
            half_c = sb(e1, "half_c", [128, 1])

            with ExitStack() as e1s:
                mu_bc = sb(e1s, "mu_bc", [128, 1792])
                omm_bc = sb(e1s, "omm_bc", [128, 1792])
                stg = [sb(e1s, f"stg{i}", [128, 1792]) for i in range(2)]
                posi = sb(e1s, "posi", [128, NB], I32)
                posf = sb(e1s, "posf", [128, NB])
                invf = sb(e1s, "invf", [128, 8])
                ang = sb(e1s, "ang", [128, NB * 8])
                ki = sb(e1s, "ki", [128, NB * 8], I32)
                kf = sb(e1s, "kf", [128, NB * 8])
                mk_ = sb(e1s, "mk_", [128, NB * 8])
                ang2 = sb(e1s, "ang2", [128, NB * 8])
                t4 = sb(e1s, "t4", [128, 4])
                DMA("sp", mu_bc[:], mu_d.to_broadcast([128, 1792]), (), ["mu_bc"], "c_mu")
                TS("dve", omm_bc[:], mu_bc[:], -1.0, 1.0, ALU.mult, ALU.add, ["mu_bc"], ["omm_bc"])
                for c in range(8):
                    sp_ = c % 2
                    DMA("act", stg[sp_][:], w_in_d[c * 128:(c + 1) * 128, 0:1792], (), [f"stg{sp_}"], f"ld_stg{sp_}")
                    TT("dve", wS[:, c, :], stg[sp_][:], mu_bc[:], ALU.mult, [f"stg{sp_}", "mu_bc"], ["wS"])
                    TT("pool", wA[:, c, 0:1792], stg[sp_][:], omm_bc[:], ALU.mult, [f"stg{sp_}", "omm_bc"], ["wA"])
                    for a in range(2):
                        DMA("pool", wA[:, c, 1792:2304].rearrange("p (h a d) -> p h a d", h=4, a=2)[:, :, a, :],
                            w_in_d[c * 128:(c + 1) * 128, 1792 + a * 256:1792 + (a + 1) * 256].rearrange("p (h d) -> p h d", h=4),
                            (), [f"wA_q{c}_{a}"], "bulk")
                    DMA("pool", wA[:, c, 2304:2560], w_in_d[c * 128:(c + 1) * 128, 2304:2560], (), [f"wA_r{c}"], "bulk")
                DMA("sp", lora_up[:], lora_d, (), ["lora_up"], "bulk")
                DMA("pool", gup[:], gup_d, (), ["gup"], "bulk")
                DMA("sp", mask1[:], mask1_d, (), ["mask1"], "bulk")
                DMA("sp", masksl[:], masksl_d, (), ["masksl"], "bulk")
                DMA("pool", amask[:], amask_d, (), ["amask"], "bulk")
                DMA("sp", bones[:], bones_d, (), ["bones"], "bulk")
                DMA("sp", hind[:], hind_d, (), ["hind"], "bulk")
                DMA("sp", lng_bc[:], lng_d.to_broadcast([64, 512]), (), ["lng_bc"], "bulk")
                DMA("sp", lnb_bc[:], lnb_d.to_broadcast([64, 512]), (), ["lnb_bc"], "bulk")
                DMA("sp", qkg_bc[:], qkg_d.to_broadcast([128, 640]), (), ["qkg_bc"], "c_qkg")
                TS("dve", qkg_bc[:, 0:512], qkg_bc[:, 0:512], 0.125, None, ALU.mult, None, ["qkg_bc"], ["qkg_bc"])
                DMA("sp", esink[:], sink_d.to_broadcast([128, 8]), (), ["esink"], "c_sink")
                ACTF(esink[:], esink[:], ACT.Exp, ["esink"], ["esink"])
                DMA("sp", t4[:], w0_d, (), ["t4"], "c_t4")
                TS("dve", hw0[:], t4[:], 0.5, None, ALU.mult, None, ["t4"], ["hw0"])
                DMA("sp", t4[:], a0_d, (), ["t4"], "c_t4")
                TS("dve", ha0[:], t4[:], 0.5, None, ALU.mult, None, ["t4"], ["ha0"])
                DMA("sp", kkT[:], kk_d, (), ["kkT"], "bulk")
                DMA("sp", rkT[:], rk_d, (), ["rkT"], "bulk")
                DMA("sp", t4[:], ka_d, (), ["t4"], "c_t4")
                TS("dve", hka[:], t4[:], 0.5, None, ALU.mult, None, ["t4"], ["hka"])
                TS("dve", omhka[:], t4[:], -0.5, 1.0, ALU.mult, ALU.add, ["t4"], ["omhka"])
                MEMSET("pool", ones64[:], 1.0, ["ones64"])
                MEMSET("pool", negk[:], -KDEC, ["negk"])
                MEMSET("pool", half_c[:], 0.5, ["half_c"])
                DMA("sp", posi[:], pos_d, (), ["posi"], "c_pos")
                DMA("sp", invf[:], invf_d.to_broadcast([128, 8]), (), ["invf"], "c_invf")
                CP("dve", posf[:], posi[:], ["posi"], ["posf"])
                ang3 = ang[:].rearrange("p (n f) -> p n f", f=8)
                TT("dve", ang3, posf[:].unsqueeze(2).to_broadcast([128, NB, 8]), invf[:].unsqueeze(1).to_broadcast([128, NB, 8]),
                   ALU.mult, ["posf", "invf"], ["ang"])
                TWO_PI = float(2 * np.pi)
                PI = float(np.pi)
                C1 = float(np.float32(6.28125))
                C2 = float(2 * np.pi - 6.28125)

                def wrap(t, key):
                    TS("dve", mk_[:], t, PI, -TWO_PI, ALU.is_gt, ALU.mult, [key], ["mk_"])
                    TT("dve", t, t, mk_[:], ALU.add, [key, "mk_"], [key])
                    TS("dve", mk_[:], t, -PI, TWO_PI, ALU.is_lt, ALU.mult, [key], ["mk_"])
                    TT("dve", t, t, mk_[:], ALU.add, [key, "mk_"], [key])

                TS("dve", ki[:], ang[:], 1.0 / TWO_PI, None, ALU.mult, None, ["ang"], ["ki"])
                CP("dve", kf[:], ki[:], ["ki"], ["kf"])
                STT(ang[:], kf[:], -C1, ang[:], ALU.mult, ALU.add, ["kf", "ang"], ["ang"])
                STT(ang[:], kf[:], -C2, ang[:], ALU.mult, ALU.add, ["kf", "ang"], ["ang"])
                wrap(ang[:], "ang")
                TS("dve", ang2[:], ang[:], PI / 2, None, ALU.add, None, ["ang"], ["ang2"])
                wrap(ang2[:], "ang2")
                TS("dve", ang[:], ang[:], PI, -PI, ALU.min, ALU.max, ["ang"], ["ang"])
                TS("dve", ang2[:], ang2[:], PI, -PI, ALU.min, ALU.max, ["ang2"], ["ang2"])
                ACTF(sin_t[:].rearrange("p n f -> p (n f)"), ang[:], ACT.Sin, ["ang"], ["sin_t"])
                ACTF(cos_t[:].rearrange("p n f -> p (n f)"), ang2[:], ACT.Sin, ["ang2"], ["cos_t"])
                with ExitStack() as e0:
                    cT = sb(e0, "cT", [128, 8, 2])
                    adab = sb(e0, "adab", [128, 48])
                    n1g = sb(e0, "n1g", [128, 8])
                    n2g = sb(e0, "n2g", [128, 8])
                    bgb = sb(e0, "bgb", [128, 16])
                    tmp8 = sb(e0, "tmp8", [128, 8])
                    adarow = sb(e0, "adarow", [2, 6 * D])
                    awp = [sb(e0, f"awp{i}", [128, 8, 1024]) for i in range(2)]
                    DMA("sp", cT[:], cT_d, (), ["cT"], "c_cT")
                    DMA("sp", adab[:], ada_b_d, (), ["adab"], "c_adab")
                    DMA("sp", n1g[:], n1g_d, (), ["n1g"], "c_n1g")
                    DMA("sp", n2g[:], n2g_d, (), ["n2g"], "c_n2g")
                    DMA("sp", bgb[:], bgb_d, (), ["bgb"], "c_bgb")
                    for g in range(6):
                        pb = g % 2
                        for kc in range(8):
                            DMA("sp", awp[pb][:, kc, :], ada_w_d[kc * 128:(kc + 1) * 128, g * 1024:(g + 1) * 1024],
                                (), [f"awp{pb}_{kc}"], f"ld_awp{pb}")
                        AW = [f"awp{pb}_{k_}" for k_ in range(8)]
                        for hb in range(2):
                            bR = bank()
                            for kc in range(8):
                                MM(ps[bR][0:2, :], cT[:, kc, :], awp[pb][:, kc, hb * 512:(hb + 1) * 512], kc == 0, kc == 7, AW + ["cT"], [PK(bR)])
                            CP("act", adarow[:, g * 1024 + hb * 512:g * 1024 + (hb + 1) * 512], ps[bR][0:2, :], [PK(bR)], ["adarow"])
                    bA = bank()
                    for j in range(48):
                        S.op("pe", lambda e, j=j: e.transpose(out=ps[bA][:, j:j + 1], in_=adarow[0:1, j * 128:(j + 1) * 128], identity=ident[0:1, 0:1]),
                             ["adarow", "ident"], [PK(bA)])
                    TT("dve", adaT[:], ps[bA][:, 0:48], adab[:], ALU.add, [PK(bA), "adab"], ["adaT"])
                    TS("dve", tmp8[:], adaT[:, 8:16], 1.0, None, ALU.add, None, ["adaT"], ["tmp8"])
                    TT("dve", gs1[:], tmp8[:], n1g[:], ALU.mult, ["tmp8", "n1g"], ["gs1"])
                    TS("dve", tmp8[:], adaT[:, 32:40], 1.0, None, ALU.add, None, ["adaT"], ["tmp8"])
                    TT("dve", gs2[:], tmp8[:], n2g[:], ALU.mult, ["tmp8", "n2g"], ["gs2"])
                    TS("dve", hbg[:], bgb[:], 0.5, None, ALU.mult, None, ["bgb"], ["hbg"])
                    if dbg:
                        DMA("sp", dbg_d["adaT"], adaT[:], ["adaT"], (), "dbg_adaT")
                    S.barrier()
                    S.emit()
                if KSTOP == "S1":
                    return nc

            xa = [sb(e1, f"xa{i}", [128, D]) for i in range(2)]
            xn = sb(e1, "xn", [128, D])
            ss = sb(e1, "ss", [128, 4])
            hT = [sb(e1, f"hT{i}", [128, 8, 129], BF16) for i in range(2)]
            rT = sb(e1, "rT", [128, 4, 128])
            kT = sb(e1, "kT", [128, 4, 128])
            txw = sb(e1, "txw", [128, 128])
            sgT2 = [sb(e1, f"sgT{i}", [128, 128], BF16) for i in range(2)]
            tg = sb(e1, "tg", [128, 128])
            vtok2 = [sb(e1, f"vtok_{i}", [64, 2, 512]) for i in range(2)]
            qkv = sb(e1, "qkv", [128, 768])
            sqt = sb(e1, "sqt", [128, 640])
            ssq = sb(e1, "ssq", [128, 16])
            rq = sb(e1, "rq", [128, 16])
            rt4 = sb(e1, "rt4", [128, 4, 80])
            qT = sb(e1, "qT", [128, 4, 128], BF16)
            kTb = sb(e1, "kTb", [128, 2, 128], BF16)
            vaug = sb(e1, "vaug", [128, 2, 2, 65], BF16)
            eT = sb(e1, "eT", [128, 2, 512], BF16)
            den = sb(e1, "den", [128, 8])
            yb_tok = sb(e1, "yb_tok", [128, 512])
            ybT = sb(e1, "ybT", [128, 4, 128], BF16)
            tw = sb(e1, "tw", [128, 128])
            lw = sb(e1, "lw", [128, 128])
            ta4 = sb(e1, "ta4", [128, 4, 128])
            kk2 = sb(e1, "kk2", [128, 128])
            ssk4 = sb(e1, "ssk4", [128, 4, 128])
            kkn = sb(e1, "kkn", [128, 128])
            t1 = sb(e1, "t1", [128, 128])
            kp = sb(e1, "kp", [128, 128])
            a_ = sb(e1, "a_", [128, 128])
            bb = sb(e1, "bb", [128, 128])
            Lc = sb(e1, "Lc", [128, 128])
            enL4 = sb(e1, "enL4", [128, 4, 128])
            rk4 = sb(e1, "rk4", [128, 4, 128])
            AR2 = [[sb(e1, f"AR{c}_{p}", [128, 2, 2, 64], SDT) for c in range(4)] for p in range(2)]
            btT2 = [[sb(e1, f"btT{c}_{p}", [128, 128], SDT) for c in range(4)] for p in range(2)]
            ktT2 = [[sb(e1, f"ktT{c}_{p}", [128, 128], SDT) for c in range(4)] for p in range(2)]
            BbT2 = [[sb(e1, f"BbT{c}_{p}", [128, 128]) for c in range(4)] for p in range(2)]
            KbT2 = [[sb(e1, f"KbT{c}_{p}", [128, 128]) for c in range(4)] for p in range(2)]
            eLx2 = [[sb(e1, f"eLx{c}_{p}", [128, 2, 65]) for c in range(4)] for p in range(2)]
            bon2 = [sb(e1, f"bon_{p}", [64, 2, 8]) for p in range(2)]
            A1s2 = [sb(e1, f"A1s_{j}", [64, 8, 128], SDT) for j in range(2)]
            A2s2 = [sb(e1, f"A2s_{j}", [64, 8, 128], SDT) for j in range(2)]
            Pm2 = [sb(e1, f"Pm_{j}", [64, 8, 64], SDT) for j in range(2)]
            PTm2 = [sb(e1, f"PTm_{j}", [64, 8, 64], SDT) for j in range(2)]
            TTm2 = [sb(e1, f"TTm_{j}", [64, 8, 64], SDT) for j in range(2)]
            Xs = sb(e1, "Xs", [64, 8, 64], SDT)
            Us = sb(e1, "Us", [64, 8, 64], SDT)
            Btok2 = [sb(e1, f"Btok_{j}", [64, 4, 128], SDT) for j in range(2)]
            Ktok2 = [sb(e1, f"Ktok_{j}", [64, 4, 128], SDT) for j in range(2)]
            Hb = sb(e1, "Hb", [128, 4, 128], SDT)
            vtokb2 = [sb(e1, f"vtokb_{i}", [64, 2, 512], SDT) for i in range(2)]
            Hblk = sb(e1, "Hblk", [128, 4, 128])
            ysb2 = [sb(e1, f"ysb_{j}", [64, 512]) for j in range(2)]
            ysq = sb(e1, "ysq", [64, 512])
            yc = sb(e1, "yc", [64, 512])
            bv = sb(e1, "bv", [64, 512])
            s1 = sb(e1, "s1", [64, 8])
            s2 = sb(e1, "s2", [64, 8])
            mean = sb(e1, "mean", [64, 8])
            msq = sb(e1, "msq", [64, 8])
            var = sb(e1, "var", [64, 8])
            rstd = sb(e1, "rstd", [64, 8])
            ya_tok = sb(e1, "ya_tok", [64, 512])
            yaT = sb(e1, "yaT", [128, 4, 128], BF16)

            MEMSET("pool", Hblk[:], 0.0, ["Hst"])
            MEMSET("pool", Hb[:], 0.0, ["Hb"])
            MEMSET("pool", vaug[:], 1.0, ["vaug"])
            for p_ in range(2):
                for c in range(4):
                    MEMSET("pool", eLx2[p_][c][:], 1.0, [f"eLx{c}_{p_}"])
            MEMSET("pool", hT[0][:], 0.0, ["hT0"])
            MEMSET("pool", hT[1][:], 0.0, ["hT1"])

            def front(n):
                par = n % 2
                slot, pslot = n % 2, (n + 1) % 2
                XA, HT = f"xa{par}", f"hT{par}"
                vtok, vtokb, sgT = vtok2[par], vtokb2[par], sgT2[par]
                VK = f"_{par}"
                DMA("sp", xa[par][:], x_d[n * 128:(n + 1) * 128, :], (), [XA], f"ldx{par}")
                ACTF(xn[:], xa[par][:], ACT.Square, [XA], ["xn", "ss"], accum=ss[:, 0:1])
                TS("dve", ss[:, 1:2], ss[:, 0:1], 1.0 / D, RMS_EPS, ALU.mult, ALU.add, ["ss"], ["ss1"])
                rsqrt_pool(ss[:, 2:3], ss[:, 1:2], ["ss1"], ["ss2"])
                ACTF(xn[:], xa[par][:], ACT.Identity, [XA, "ss2", "ss"], ["xn"], scale=ss[:, 2:3])
                yield
                if n > 0:
                    CP("pool", hT[par][:, :, 0:1], hT[1 - par][:, :, 128:129], [f"hT{1 - par}"], [HT])
                for half in range(2):
                    b = fbank()
                    for q in range(4):
                        c = half * 4 + q
                        TR(ps[b][:, q * 128:(q + 1) * 128], xn[:, c * 128:(c + 1) * 128], ["xn"], [PK(b)])
                    for q in range(4):
                        c = half * 4 + q
                        if q % 2 == 0:
                            ACTF(hT[par][:, c, 1:129], ps[b][:, q * 128:(q + 1) * 128], ACT.Identity, [PK(b), "gs1", "adaT"], [HT],
                                 scale=gs1[:, c:c + 1], bias=adaT[:, c:c + 1])
                        else:
                            TS("dve", hT[par][:, c, 1:129], ps[b][:, q * 128:(q + 1) * 128], gs1[:, c:c + 1], adaT[:, c:c + 1],
                               ALU.mult, ALU.add, [PK(b), "gs1", "adaT"], [HT])

                yield
                def fm_group(col0, out):
                    b_ = out[0]
                    for c in range(8):
                        MM(out[1], wA[:, c, col0:col0 + 128], hT[par][:, c, 1:129], c == 0, False, ["wA", HT], [PK(b_)])
                    for c in range(8):
                        MM(out[1], wS[:, c, col0:col0 + 128], hT[par][:, c, 0:128], False, c == 7, ["wS", HT], [PK(b_)])

                b = fbank()
                for q in range(4):
                    fm_group(q * 128, (b, ps[b][:, q * 128:(q + 1) * 128]))
                    yield
                CP("act", rT[:].rearrange("p c t -> p (c t)"), ps[b][:, :], [PK(b)], ["rT"])
                yield
                b = fbank()
                for q in range(4):
                    fm_group(512 + q * 128, (b, ps[b][:, q * 128:(q + 1) * 128]))
                    yield
                CP("dve", kT[:].rearrange("p c t -> p (c t)"), ps[b][:, :], [PK(b)], ["kT"])
                yield
                b = fbank()
                fm_group(1536, (b, ps[b][:, 0:128]))
                yield
                fm_group(1664, (b, ps[b][:, 128:256]))
                ACTF(txw[0:64, :], ps[b][0:64, 0:128], ACT.Tanh, [PK(b)], ["txw"])
                ACTF(txw[64:128, :], ps[b][64:128, 0:128], ACT.Identity, [PK(b)], ["txw"])
                ACTF(tg[:], ps[b][:, 128:256], ACT.Tanh, [PK(b)], ["tg"], scale=0.5)
                TS("dve", sgT[:], tg[:], 0.5, 0.5, ALU.mult, ALU.add, ["tg"], ["sgT" + VK])

                yield
                for j in range(2):
                    b = fbank()
                    for c in range(8):
                        MM(ps[b][0:64, :], hT[par][:, c, 1 + 64 * j:65 + 64 * j], wA[:, c, 1024:1536], c == 0, False, ["wA", HT], [PK(b)])
                    for c in range(8):
                        MM(ps[b][0:64, :], hT[par][:, c, 64 * j:64 * j + 64], wS[:, c, 1024:1536], False, c == 7, ["wS", HT], [PK(b)])
                    CP("act", vtok[:, j, :], ps[b][0:64, :], [PK(b)], [f"vtok{j}" + VK])
                    CP("dve", vtokb[:, j, :], vtok[:, j, :], [f"vtok{j}" + VK], [f"vtokb{j}" + VK])
                    yield
                b = fbank()
                for c in range(8):
                    MM(ps[b][:, :], hT[par][:, c, 1:129], wA[:, c, 1792:2304], c == 0, c == 7, ["wA", HT], [PK(b)])
                CP("act", qkv[:, 0:512], ps[b][:, :], [PK(b)], ["qkv"])
                yield
                b = fbank()
                for c in range(8):
                    MM(ps[b][:, 0:256], hT[par][:, c, 1:129], wA[:, c, 2304:2560], c == 0, c == 7, ["wA", HT], [PK(b)])
                CP("dve", qkv[:, 512:768], ps[b][:, 0:256], [PK(b)], ["qkv"])

                yield
                qk3 = qkv[:, 0:640].rearrange("p (h d) -> p h d", d=64)
                ACTF(sqt[:], qkv[:, 0:640], ACT.Square, ["qkv"], ["sqt"])
                S.op("dve", lambda e: e.tensor_reduce(out=ssq[:, 0:10], in_=sqt[:].rearrange("p (h d) -> p h d", d=64), axis=AX.X, op=ALU.add),
                     ["sqt"], ["ssq"])
                TS("dve", ssq[:, 0:10], ssq[:, 0:10], 1.0 / 64, RMS_EPS, ALU.mult, ALU.add, ["ssq"], ["ssq"])
                TT("pool", rq[:, 0:10], ssq[:, 0:10], cm05[:, 0:10], ALU.pow, ["ssq", "cm05"], ["rq"])
                TT("dve", qk3, qk3, rq[:, 0:10].unsqueeze(2).to_broadcast([128, 10, 64]), ALU.mult, ["qkv", "rq"], ["qkv"])
                TT("dve", qkv[:, 0:640], qkv[:, 0:640], qkg_bc[:], ALU.mult, ["qkv", "qkg_bc"], ["qkv"])
                x1v, x2v = qk3[:, :, 0:8], qk3[:, :, 8:16]
                cosb = cos_t[:, n, :].unsqueeze(1).to_broadcast([128, 10, 8])
                sinb = sin_t[:, n, :].unsqueeze(1).to_broadcast([128, 10, 8])
                r4 = [rt4[:, i, :].rearrange("p (h f) -> p h f", f=8) for i in range(4)]
                TT("dve", r4[0], x1v, cosb, ALU.mult, ["qkv", "cos_t"], ["rt4_0"])
                TT("dve", r4[1], x2v, sinb, ALU.mult, ["qkv", "sin_t"], ["rt4_1"])
                TT("dve", r4[2], x2v, cosb, ALU.mult, ["qkv", "cos_t"], ["rt4_2"])
                TT("dve", r4[3], x1v, sinb, ALU.mult, ["qkv", "sin_t"], ["rt4_3"])
                TT("dve", x1v, r4[0], r4[1], ALU.subtract, ["rt4_0", "rt4_1"], ["qkv"])
                TT("dve", x2v, r4[2], r4[3], ALU.add, ["rt4_2", "rt4_3"], ["qkv"])
                yield
                b = fbank()
                for h in range(4):
                    TR(ps[b][:, h * 128:(h + 1) * 128], qkv[:, h * 128:(h + 1) * 128], ["qkv"], [PK(b)])
                CP("act", qT[:].rearrange("p h t -> p (h t)"), ps[b][:, :], [PK(b)], ["qT"])
                b = fbank()
                TR(ps[b][:, 0:128], qkv[:, 512:640], ["qkv"], [PK(b)])
                CP("dve", kTb[:, slot, :], ps[b][:, 0:128], [PK(b)], [f"kTb{slot}"])
                CP("pool", vaug[:, slot, :, 0:64], qkv[:, 640:768].rearrange("p (g d) -> p g d", d=64), ["qkv"], [f"vaug{slot}"])

                yield
                for g in range(2):
                    pr = slice(64 * g, 64 * g + 64)
                    bc_ = fbank()
                    MM(ps[bc_][:, :], kTb[pr, slot, :], qT[pr, :, :], True, True, [f"kTb{slot}", "qT"], [PK(bc_)])
                    ACTF(eT[:, 1, :], ps[bc_][:, :], ACT.Exp, [PK(bc_)], ["eT1"])
                    TT("dve", eT[:, 1, :].rearrange("p (h t) -> p h t", h=4), eT[:, 1, :].rearrange("p (h t) -> p h t", h=4),
                       amask[:, 1, :].unsqueeze(1).to_broadcast([128, 4, 128]), ALU.mult, ["eT1", "amask"], ["eT1"])
                    if n > 0:
                        bp_ = fbank()
                        MM(ps[bp_][:, :], kTb[pr, pslot, :], qT[pr, :, :], True, True, [f"kTb{pslot}", "qT"], [PK(bp_)])
                        ACTF(eT[:, 0, :], ps[bp_][:, :], ACT.Exp, [PK(bp_)], ["eT0"])
                        TT("dve", eT[:, 0, :].rearrange("p (h t) -> p h t", h=4), eT[:, 0, :].rearrange("p (h t) -> p h t", h=4),
                           amask[:, 0, :].unsqueeze(1).to_broadcast([128, 4, 128]), ALU.mult, ["eT0", "amask"], ["eT0"])
                    yield
                    bo = fbank()
                    for jq in range(4):
                        o = ps[bo][:, jq * 65:(jq + 1) * 65]
                        if n > 0:
                            MM(o, eT[:, 0, jq * 128:(jq + 1) * 128], vaug[:, pslot, g, :], True, False, ["eT0", f"vaug{pslot}"], [PK(bo)])
                        MM(o, eT[:, 1, jq * 128:(jq + 1) * 128], vaug[:, slot, g, :], n == 0, True, ["eT1", f"vaug{slot}"], [PK(bo)])
                    pv3 = ps[bo][:, 0:260].rearrange("p (j e) -> p j e", e=65)
                    TT("dve", den[:, g * 4:(g + 1) * 4], pv3[:, :, 64], esink[:, g * 4:(g + 1) * 4], ALU.add, [PK(bo), "esink"], ["den"])
                    S.op("dve", lambda e, g=g: e.reciprocal(out=den[:, g * 4:(g + 1) * 4], in_=den[:, g * 4:(g + 1) * 4]), ["den"], ["den"])
                    TT("dve", yb_tok[:, g * 256:(g + 1) * 256].rearrange("p (j d) -> p j d", d=64), pv3[:, :, 0:64],
                       den[:, g * 4:(g + 1) * 4].unsqueeze(2).to_broadcast([128, 4, 64]), ALU.mult, [PK(bo), "den"], ["yb_tok"])
                yield
                if dbg:
                    DMA("sp", dbg_d["yb_tok"][n], yb_tok[:], ["yb_tok"], (), "dbg_yb")
                b = fbank()
                for c in range(4):
                    TR(ps[b][:, c * 128:(c + 1) * 128], yb_tok[:, c * 128:(c + 1) * 128], ["yb_tok"], [PK(b)])
                CP("act", ybT[:].rearrange("p c t -> p (c t)"), ps[b][:, :], [PK(b)], ["ybT"])
                DMA("sp", ysp_d[1, :, n], ybT[:], ["ybT"], (), "st_ybT")

                yield

            def stage8(n):
                par = n % 2
                AR, btT, ktT, BbT, KbT, eLx, bon = AR2[par], btT2[par], ktT2[par], BbT2[par], KbT2[par], eLx2[par], bon2[par]
                PS = f"_{par}"
                for c in range(4):
                    cs = slice(c * 128, (c + 1) * 128)
                    b, bq = fbank(), fbank()
                    MM(ps[b][:, 0:128], lora_up[0:64, cs], txw[0:64, :], True, True, ["lora_up", "txw"], [PK(b)])
                    MM(ps[bq][:, 128:256], lora_up[64:128, cs], txw[64:128, :], True, True, ["lora_up", "txw"], [PK(bq)])
                    ACTF(tw[:], ps[b][:, 0:128], ACT.Tanh, [PK(b), "hw0"], ["tw"], scale=0.5, bias=hw0[:, c:c + 1])
                    ACTF(ta4[:, c, :], ps[bq][:, 128:256], ACT.Tanh, [PK(bq), "ha0"], [f"ta4_{c}"], scale=0.5, bias=ha0[:, c:c + 1])
                    ACTF(lw[:], tw[:], ACT.Identity, ["tw"], ["lw"], scale=-KDEC, bias=negk[:, 0:1])
                    yield
                    ACTF(kk2[:], kT[:, c, :], ACT.Square, ["kT", "kkT"], ["kk2"], scale=kkT[:, c:c + 1])
                    b2 = fbank()
                    MM(ps[b2][:, 0:128], bones[:], kk2[:], True, True, ["bones", "kk2"], [PK(b2)])
                    TS("dve", ssk4[:, c, :], ps[b2][:, 0:128], 1e-24, None, ALU.add, None, [PK(b2)], [f"ssk4_{c}"])
                    ACTF(t1[:], ta4[:, c, :], ACT.Identity, [f"ta4_{c}", "hka", "omhka"], ["t1"], scale=hka[:, c:c + 1], bias=omhka[:, c:c + 1])
                    TT("pool", kp[:], t1[:], kT[:, c, :], ALU.mult, ["t1", "kT"], ["kp"])
                    for j in range(2):
                        js = slice(j * 64, (j + 1) * 64)
                        S.op("dve", lambda e, js=js: e.tensor_tensor_scan(out=Lc[:, js], data0=ones64[:], data1=lw[:, js], initial=0.0,
                                                                         op0=ALU.mult, op1=ALU.add), ["lw", "ones64"], ["Lc"])
                    yield
                    L3 = Lc[:].rearrange("p (j t) -> p j t", t=64)
                    ACTF(eLx[c][:, :, 1:65], L3, ACT.Exp, ["Lc"], [f"eLx{c}" + PS])
                    ACTF(enL4[:, c, :], Lc[:], ACT.Exp, ["Lc"], [f"enL4_{c}"], scale=-1.0)
                    TT("dve", AR[c][:, :, 1, :], rT[:, c, :].rearrange("p (j t) -> p j t", t=64), eLx[c][:, :, 1:65], ALU.mult,
                       ["rT", f"eLx{c}" + PS], [f"AR{c}" + PS])
                    TT("pool", ktT[c][:], kp[:], enL4[:, c, :], ALU.mult, ["kp", f"enL4_{c}"], [f"ktT{c}" + PS])
                    for j in range(2):
                        js = slice(j * 64, (j + 1) * 64)
                        ACTF(KbT[c][:, js], ktT[c][:, js], ACT.Identity, [f"ktT{c}" + PS, f"eLx{c}" + PS], [f"KbT{c}" + PS], scale=eLx[c][:, j, 64:65])
                    STT(rk4[:, c, :], rT[:, c, :], rkT[:, c:c + 1], kp[:], ALU.mult, ALU.mult, ["rT", "rkT", "kp"], ["rk4"])
                    yield
                ssk_flat = ssk4[:].rearrange("p c t -> p (c t)")
                SSK = [f"ssk4_{c}" for c in range(4)]
                ACTF(ssk_flat, ssk_flat, ACT.Ln, SSK, SSK)
                ACTF(ssk_flat, ssk_flat, ACT.Exp, SSK, SSK, scale=-0.5)
                for c in range(4):
                    STT(kkn[:], kT[:, c, :], kkT[:, c:c + 1], ssk4[:, c, :], ALU.mult, ALU.mult, ["kT", "kkT", f"ssk4_{c}"], ["kkn"])
                    ACTF(a_[:], ta4[:, c, :], ACT.Identity, [f"ta4_{c}"], ["a_"], scale=0.5, bias=half_c[:, 0:1])
                    TT("pool", bb[:], a_[:], kkn[:], ALU.mult, ["a_", "kkn"], ["bb"])
                    STT(AR[c][:, :, 0, :], kkn[:].rearrange("p (j t) -> p j t", t=64), -1.0, eLx[c][:, :, 0:64], ALU.mult, ALU.mult,
                        ["kkn", f"eLx{c}" + PS], [f"AR{c}" + PS])
                    TT("pool", btT[c][:], bb[:], enL4[:, c, :], ALU.mult, ["bb", f"enL4_{c}"], [f"btT{c}" + PS])
                    for j in range(2):
                        js = slice(j * 64, (j + 1) * 64)
                        ACTF(BbT[c][:, js], btT[c][:, js], ACT.Identity, [f"btT{c}" + PS, f"eLx{c}" + PS], [f"BbT{c}" + PS], scale=eLx[c][:, j, 64:65])
                    yield
                bB = fbank()
                for c in range(4):
                    for j in range(2):
                        MM(ps[bB][0:64, j * 8 + 2 * c:j * 8 + 2 * c + 2], rk4[:, c, j * 64:(j + 1) * 64], hind[:], True, True, ["rk4", "hind"], [PK(bB)])
                CP("act", bon[:].rearrange("p j h -> p (j h)"), ps[bB][0:64, 0:16], [PK(bB)], ["bon" + PS])

                yield

            def scan_post(n):
                par = n % 2
                vtok, vtokb, sgT = vtok2[par], vtokb2[par], sgT2[par]
                AR, btT, ktT, BbT, KbT, eLx, bon = AR2[par], btT2[par], ktT2[par], BbT2[par], KbT2[par], eLx2[par], bon2[par]
                VK = f"_{par}"
                PS = f"_{par}"

                def hp(h):
                    return h // 2, slice(64 * (h % 2), 64 * (h % 2) + 64)

                def slot_(h):
                    return (h % 2) * 4 + h // 2

                m1b = mask1[:].unsqueeze(1).to_broadcast([64, 4, 128])
                for j in range(2):
                    js = slice(j * 64, (j + 1) * 64)
                    A1s, A2s, Pm, TTm = A1s2[j], A2s2[j], Pm2[j], TTm2[j]
                    A1K, A2K, PK_, TK = f"A1s{j}", f"A2s{j}", f"Pm{j}", f"TTm{j}"
                    bA1 = [sbank(), sbank()]
                    for h in range(8):
                        c, pr = hp(h)
                        MM(ps[bA1[h % 2]][0:64, (h // 2) * 128:(h // 2 + 1) * 128], btT[c][pr, js], AR[c][pr, j, :, :], True, True,
                           [f"btT{c}" + PS, f"AR{c}" + PS], [PK(bA1[h % 2])])
                    for q in range(2):
                        TT("dve", A1s[:, q * 4:(q + 1) * 4, :], ps[bA1[q]][0:64, :].rearrange("p (h t) -> p h t", h=4), m1b, ALU.mult,
                           [PK(bA1[q]), "mask1"], [A1K])
                    yield
                    bA2 = [sbank(), sbank()]
                    for h in range(8):
                        c, pr = hp(h)
                        MM(ps[bA2[h % 2]][0:64, (h // 2) * 128:(h // 2 + 1) * 128], ktT[c][pr, js], AR[c][pr, j, :, :], True, True,
                           [f"ktT{c}" + PS, f"AR{c}" + PS], [PK(bA2[h % 2])])
                    for q in range(2):
                        TT("dve", A2s[:, q * 4:(q + 1) * 4, :], ps[bA2[q]][0:64, :].rearrange("p (h t) -> p h t", h=4), m1b, ALU.mult,
                           [PK(bA2[q]), "mask1"], [A2K])
                    yield
                    bA3 = [sbank(), sbank()]
                    for h in range(8):
                        c, pr = hp(h)
                        MM(ps[bA3[h % 2]][0:64, (h // 2) * 64:(h // 2 + 1) * 64], AR[c][pr, j, 0, :], btT[c][pr, js], True, True,
                           [f"btT{c}" + PS, f"AR{c}" + PS], [PK(bA3[h % 2])])
                    for q in range(2):
                        TT("dve", Pm[:, q * 4:(q + 1) * 4, :], ps[bA3[q]][0:64, 0:256].rearrange("p (h t) -> p h t", h=4),
                           masksl[:].unsqueeze(1).to_broadcast([64, 4, 64]), ALU.mult, [PK(bA3[q]), "masksl"], [PK_])
                    TT("pool", TTm[:], A1s[:, :, 0:64], ident[0:64, 0:64].unsqueeze(1).to_broadcast([64, 8, 64]), ALU.add,
                       [A1K, "ident"], [TK])
                    yield
                    bBt, bKt = sbank(), sbank()
                    for c in range(4):
                        TR(ps[bBt][0:64, c * 128:(c + 1) * 128], BbT[c][:, js], [f"BbT{c}" + PS], [PK(bBt)])
                    for c in range(4):
                        TR(ps[bKt][0:64, c * 128:(c + 1) * 128], KbT[c][:, js], [f"KbT{c}" + PS], [PK(bKt)])
                    CP("act", Btok2[j][:].rearrange("p c t -> p (c t)"), ps[bBt][0:64, :], [PK(bBt)], [f"Btok{j}"])
                    CP("act", Ktok2[j][:].rearrange("p c t -> p (c t)"), ps[bKt][0:64, :], [PK(bKt)], [f"Ktok{j}"])
                    yield
                PTcur = [A1s2[0][:, :, 0:64], A1s2[1][:, :, 0:64]]
                PTkey = ["A1s0", "A1s1"]
                for l in range(1, 6):
                    for j in range(2):
                        Pm, PTm, TTm = Pm2[j], PTm2[j], TTm2[j]
                        PK_, PTK, TK = f"Pm{j}", f"PTm{j}", f"TTm{j}"
                        bP = sbank()
                        for h in range(8):
                            MM(ps[bP][0:64, h * 64:(h + 1) * 64], PTcur[j][:, h, :], Pm[:, h, :], True, True, [PTkey[j], PK_], [PK(bP)])
                        if l < 5:
                            bPT = sbank()
                            for h in range(8):
                                MM(ps[bPT][0:64, h * 64:(h + 1) * 64], Pm[:, h, :], PTcur[j][:, h, :], True, True, [PTkey[j], PK_], [PK(bPT)])
                        CP("act", Pm[:].rearrange("p h t -> p (h t)"), ps[bP][0:64, :], [PK(bP)], [PK_])
                        if l < 5:
                            CP("act", PTm[:].rearrange("p h t -> p (h t)"), ps[bPT][0:64, :], [PK(bPT)], [PTK])
                            PTcur[j], PTkey[j] = PTm[:], PTK
                        yield
                    for j in range(2):
                        Pm, TTm = Pm2[j], TTm2[j]
                        PK_, TK = f"Pm{j}", f"TTm{j}"
                        bT = sbank()
                        for h in range(8):
                            MM(ps[bT][0:64, h * 64:(h + 1) * 64], Pm[:, h, :], TTm[:, h, :], True, True, [PK_, TK], [PK(bT)])
                        TT("dve", TTm[:].rearrange("p h t -> p (h t)"), TTm[:].rearrange("p h t -> p (h t)"), ps[bT][0:64, :], ALU.add,
                           [TK, PK(bT)], [TK])
                        yield
                for j in range(2):
                    js = slice(j * 64, (j + 1) * 64)
                    VT = f"vtok{j}" + VK
                    VTB = f"vtokb{j}" + VK
                    A1s, A2s, TTm, Btok, Ktok, ysb = A1s2[j], A2s2[j], TTm2[j], Btok2[j], Ktok2[j], ysb2[j]
                    A1K, A2K, TK, BK, KK, YK = f"A1s{j}", f"A2s{j}", f"TTm{j}", f"Btok{j}", f"Ktok{j}", f"ysb{j}"
                    bX = sbank()
                    for c in range(4):
                        o = ps[bX][0:64, c * 128:(c + 1) * 128]
                        MM(o, AR[c][:, j, 0, :], Hb[:, c, :], True, False, [f"AR{c}" + PS, "Hb"], [PK(bX)])
                        for i in range(2):
                            h = 2 * c + i
                            MM(ps[bX][0:64, h * 64:(h + 1) * 64], A2s[:, slot_(h), 0:64], vtokb[:, j, h * 64:(h + 1) * 64], False, i == 1,
                               [A2K, VTB], [PK(bX)])
                    CP("act", Xs[:].rearrange("p h t -> p (h t)"), ps[bX][0:64, :], [PK(bX)], ["Xs"])
                    yield
                    bU = sbank()
                    for h in range(8):
                        MM(ps[bU][0:64, h * 64:(h + 1) * 64], TTm[:, slot_(h), :], Xs[:, h, :], True, True, [TK, "Xs"], [PK(bU)])
                    CP("act", Us[:].rearrange("p h t -> p (h t)"), ps[bU][0:64, :], [PK(bU)], ["Us"])
                    yield
                    bH = sbank()
                    for h in range(8):
                        c, pr = hp(h)
                        o = ps[bH][pr, c * 64:(c + 1) * 64]
                        MM(o, Btok[:, c, pr], Us[:, h, :], True, False, [BK, "Us"], [PK(bH)])
                        MM(o, Ktok[:, c, pr], vtokb[:, j, h * 64:(h + 1) * 64], False, True, [KK, VTB], [PK(bH)])
                    bY = sbank()
                    for c in range(4):
                        o = ps[bY][0:64, c * 128:(c + 1) * 128]
                        MM(o, AR[c][:, j, 1, :], Hb[:, c, :], True, False, [f"AR{c}" + PS, "Hb"], [PK(bY)])
                        for i in range(2):
                            h = 2 * c + i
                            oh = ps[bY][0:64, h * 64:(h + 1) * 64]
                            MM(oh, A1s[:, slot_(h), 64:128], Us[:, h, :], False, False, [A1K, "Us"], [PK(bY)])
                            MM(oh, A2s[:, slot_(h), 64:128], vtokb[:, j, h * 64:(h + 1) * 64], False, i == 1, [A2K, VTB], [PK(bY)])
                    for c in range(4):
                        for i in range(2):
                            pr = slice(64 * i, 64 * i + 64)
                            STT(Hblk[pr, c, i * 64:(i + 1) * 64], Hblk[pr, c, i * 64:(i + 1) * 64], eLx[c][pr, j, 64:65],
                                ps[bH][pr, c * 64:(c + 1) * 64], ALU.mult, ALU.add, ["Hst", f"eLx{c}" + PS, PK(bH)], ["Hst"])
                    CP("dve", Hb[:].rearrange("p c v -> p (c v)"), Hblk[:].rearrange("p c v -> p (c v)"), ["Hst"], ["Hb"])
                    CP("act", ysb[:], ps[bY][0:64, :], [PK(bY)], [YK])
                    if dbg:
                        DMA("sp", dbg_d["yscan"][n, j], ysb[:], [YK], (), "dbg_ys")
                    yield
                for j in range(2):
                    js = slice(j * 64, (j + 1) * 64)
                    VT = f"vtok{j}" + VK
                    ysb = ysb2[j]
                    YK = f"ysb{j}"
                    y3 = ysb[:].rearrange("p (h d) -> p h d", d=64)
                    S.op("dve", lambda e, y3=y3: e.tensor_reduce(out=s1[:], in_=y3, axis=AX.X, op=ALU.add), [YK], ["s1"])
                    ACTF(ysq[:], ysb[:], ACT.Square, [YK], ["ysq"])
                    S.op("dve", lambda e: e.tensor_reduce(out=s2[:], in_=ysq[:].rearrange("p (h d) -> p h d", d=64), axis=AX.X, op=ALU.add),
                         ["ysq"], ["s2"])
                    TS("dve", mean[:], s1[:], 1.0 / 64, None, ALU.mult, None, ["s1"], ["mean"])
                    TT("dve", msq[:], mean[:], mean[:], ALU.mult, ["mean"], ["msq"])
                    STT(var[:], s2[:], 1.0 / 64, msq[:], ALU.mult, ALU.subtract, ["s2", "msq"], ["var"])
                    TS("dve", var[:], var[:], GN_EPS, None, ALU.add, None, ["var"], ["var"])
                    TT("pool", rstd[:], var[:], cm05[0:64, 0:8], ALU.pow, ["var", "cm05"], ["rstd"])
                    yield
                    yc3 = yc[:].rearrange("p (h d) -> p h d", d=64)
                    TT("dve", yc3, y3, mean[:].unsqueeze(2).to_broadcast([64, 8, 64]), ALU.subtract, [YK, "mean"], ["yc"])
                    TT("dve", yc3, yc3, rstd[:].unsqueeze(2).to_broadcast([64, 8, 64]), ALU.mult, ["yc", "rstd"], ["yc"])
                    TT("dve", yc[:], yc[:], lng_bc[:], ALU.mult, ["yc", "lng_bc"], ["yc"])
                    TT("pool", bv[:].rearrange("p (h d) -> p h d", d=64), vtok[:, j, :].rearrange("p (h d) -> p h d", d=64),
                       bon[:, j, :].unsqueeze(2).to_broadcast([64, 8, 64]), ALU.mult, [VT, "bon" + PS], ["bv"])
                    TT("pool", bv[:], bv[:], lnb_bc[:], ALU.add, ["bv", "lnb_bc"], ["bv"])
                    TT("pool", yc[:], yc[:], bv[:], ALU.add, ["yc", "bv"], ["yc"])
                    yield
                    bg_ = sbank()
                    MM(ps[bg_][0:64, :], sgT[:, js], gup[:], True, True, ["sgT" + VK, "gup"], [PK(bg_)])
                    TT("dve", ya_tok[:], yc[:], ps[bg_][0:64, :], ALU.mult, ["yc", PK(bg_)], ["ya_tok"])
                    if dbg:
                        DMA("sp", dbg_d["ya_tok"][n, j], ya_tok[:], ["ya_tok"], (), "dbg_ya")
                    bt_ = sbank()
                    for c in range(4):
                        TR(ps[bt_][:, c * 64:(c + 1) * 64], ya_tok[:, c * 128:(c + 1) * 128], ["ya_tok"], [PK(bt_)], k=64)
                    CP("act", yaT[:, :, js], ps[bt_][:, 0:256].rearrange("p (c t) -> p c t", t=64), [PK(bt_)], ["yaT"])
                    yield
                DMA("sp", ysp_d[0, :, n], yaT[:], ["yaT"], (), "st_yaT")
                yield

            def run_all(g):
                for _ in g:
                    pass

            def interleave(g1, g2, r1=2, r2=1):
                a1 = a2 = True
                while a1 or a2:
                    for _ in range(r1):
                        if a1:
                            try:
                                next(g1)
                            except StopIteration:
                                a1 = False
                    for _ in range(r2):
                        if a2:
                            try:
                                next(g2)
                            except StopIteration:
                                a2 = False

            if _os.environ.get("K_SBUF"):
                live = {k: v for k, v in sb_bytes.items()}
                print("A1 SBUF bytes/partition (incl. persistent + freed setup temps):", sum(live.values()))
                print(sorted(live.items(), key=lambda kv: -kv[1])[:40])
            IR1 = int(_os.environ.get("K_IR1", "1"))
            IR2 = int(_os.environ.get("K_IR2", "1"))

            def chain(*gs):
                for g in gs:
                    yield from g

            run_all(chain(front(0), stage8(0)))
            for n in range(NB):
                if n + 1 < NB:
                    interleave(scan_post(n), chain(front(n + 1), stage8(n + 1)), IR1, IR2)
                else:
                    run_all(scan_post(n))
            S.barrier()
            S.emit()

        if stop_after == "A1":
            return nc

        def run_all(g):
            for _ in g:
                pass

        def interleave(g1, g2, r1=1, r2=1):
            a1 = a2 = True
            while a1 or a2:
                for _ in range(r1):
                    if a1:
                        try:
                            next(g1)
                        except StopIteration:
                            a1 = False
                for _ in range(r2):
                    if a2:
                        try:
                            next(g2)
                        except StopIteration:
                            a2 = False

        PB = [0, 1]
        MB = [2, 3, 4, 5, 6, 7]
        pctr = [0]
        mctr = [0]

        def pbank():
            b = PB[pctr[0] % len(PB)]
            pctr[0] += 1
            return b

        def mbank():
            b = MB[mctr[0] % len(MB)]
            mctr[0] += 1
            return b

        def norm_gen(e_x, hdst, gs, sh_col0, keyx, keyh, s, xn_t, ss_t, tag):
            XN, SS = "xn" + tag, "ss" + tag
            ACTF(xn_t[:], e_x, ACT.Square, [keyx], [XN, SS], accum=ss_t[:, 0:1])
            TS("dve", ss_t[:, 1:2], ss_t[:, 0:1], 1.0 / D, RMS_EPS, ALU.mult, ALU.add, [SS], [SS + "1"])
            rsqrt_pool(ss_t[:, 2:3], ss_t[:, 1:2], [SS + "1"], [SS + "2"])
            ACTF(xn_t[:], e_x, ACT.Identity, [keyx, SS + "2", SS], [XN], scale=ss_t[:, 2:3])
            yield
            for half in range(2):
                b = pbank()
                for q in range(4):
                    c = half * 4 + q
                    TR(ps[b][:, q * 128:(q + 1) * 128], xn_t[:, c * 128:(c + 1) * 128], [XN], [PK(b)])
                for q in range(4):
                    c = half * 4 + q
                    o = hdst[:, c, s * 128:(s + 1) * 128]
                    if q % 2 == 0:
                        ACTF(o, ps[b][:, q * 128:(q + 1) * 128], ACT.Identity, [PK(b), "adaT"], [keyh],
                             scale=gs[:, c:c + 1], bias=adaT[:, sh_col0 + c:sh_col0 + c + 1])
                    else:
                        TS("dve", o, ps[b][:, q * 128:(q + 1) * 128], gs[:, c:c + 1], adaT[:, sh_col0 + c:sh_col0 + c + 1],
                           ALU.mult, ALU.add, [PK(b), "adaT"], [keyh])
                yield

        g1h_bc = sb(es, "g1h_bc", [128, D])
        g2_bc = sb(es, "g2_bc", [128, D])
        with ExitStack() as eg:
            diag = sb(eg, "diag", [128, 128])
            for gi, (col0, scl, dst, key) in enumerate(((16, 0.5, g1h_bc, "g1h_bc"), (40, 1.0, g2_bc, "g2_bc"))):
                for half in range(2):
                    bG = bank()
                    for q in range(4):
                        m = half * 4 + q
                        TS("dve", diag[:], ident[:], adaT[:, col0 + m:col0 + m + 1], scl, ALU.mult, ALU.mult, ["ident", "adaT"], ["diag"])
                        MM(ps[bG][:, q * 128:(q + 1) * 128], ones_f[:], diag[:], True, True, ["ones_f", "diag"], [PK(bG)])
                    CP("act", dst[:, half * 512:(half + 1) * 512], ps[bG][:, :], [PK(bG)], [key])
            S.barrier()
            S.emit()

        TB2 = min(512, S_len)
        NT2 = S_len // TB2
        SB2 = TB2 // 128
        with ExitStack() as e2:
            wg = sb(e2, "wg", [128, 8, 2048], BF16)
            wba = sb(e2, "wba", [128, 4, D], BF16)
            wbb = sb(e2, "wbb", [128, 4, D], BF16)
            wo = sb(e2, "wo", [128, 8, D], BF16)
            with ExitStack() as e2s:
                stg2 = [sb(e2s, f"stg2_{i}", [128, D]) for i in range(2)]
                for c in range(8):
                    DMA("pool", wg[:, c, :], w_in_d[c * 128:(c + 1) * 128, 2560:4608], (), [f"wg{c}"], "bulk", max_dma_last_dim=4096)
                for c in range(4):
                    DMA("pool", wba[:, c, :], wba_d[c * 128:(c + 1) * 128, :], (), [f"wba{c}"], "bulk", max_dma_last_dim=4096)
                    DMA("pool", wbb[:, c, :], wbb_d[c * 128:(c + 1) * 128, :], (), [f"wbb{c}"], "bulk", max_dma_last_dim=4096)
                for c in range(8):
                    sp_ = c % 2
                    DMA("sp", stg2[sp_][:], wout_d[c * 128:(c + 1) * 128, :], (), [f"stg2_{sp_}"], f"ld_stg2_{sp_}")
                    TT("dve", wo[:, c, :], stg2[sp_][:], g1h_bc[:], ALU.mult, [f"stg2_{sp_}", "g1h_bc"], ["wo"])
                S.barrier()
                S.emit()
            xt2 = [sb(e2, f"xt_{i}", [128, SB2, D]) for i in range(2)]
            xn2 = sb(e2, "xn2", [128, D])
            ssb = sb(e2, "ssb", [128, 4])
            h22 = [sb(e2, f"h2_{i}", [128, 8, TB2], BF16) for i in range(2)]
            ya22 = [sb(e2, f"ya2_{i}", [128, 4, SB2, 128], BF16) for i in range(2)]
            yb22 = [sb(e2, f"yb2_{i}", [128, 4, SB2, 128], BF16) for i in range(2)]
            gta = sb(e2, "gta", [128, TB2], BF16)
            gtb = sb(e2, "gtb", [128, TB2], BF16)
            ua = sb(e2, "ua", [128, TB2])
            ub = sb(e2, "ub", [128, TB2])
            m2T = sb(e2, "m2T", [128, 8, TB2], BF16)
            x1t = [sb(e2, f"x1t{i}", [128, D]) for i in range(2)]

            def a2_prep(t):
                par = t % 2
                xt, h2, ya2, yb2 = xt2[par], h22[par], ya22[par], yb22[par]
                for s_ in range(SB2):
                    DMA("sp", xt[:, s_, :], x_d[t * TB2 + s_ * 128:t * TB2 + (s_ + 1) * 128, :], (), [f"xt{s_}_{par}"], f"ld_xt{s_}_{par}")
                for s_ in range(SB2):
                    DMA("pool", ya2[:, :, s_, :], ysp_d[0, :, t * SB2 + s_], (), [f"ya2_{s_}_{par}"], f"ld_ya2_{par}")
                    DMA("pool", yb2[:, :, s_, :], ysp_d[1, :, t * SB2 + s_], (), [f"yb2_{s_}_{par}"], f"ld_yb2_{par}")
                yield
                for s_ in range(SB2):
                    yield from norm_gen(xt[:, s_, :], h2, gs1, 0, f"xt{s_}_{par}", f"h2_{par}", s_, xn2, ssb, "A2")

            def a2_main(t):
                par = t % 2
                xt, h2, ya2, yb2 = xt2[par], h22[par], ya22[par], yb22[par]
                H2K = f"h2_{par}"
                YA = [f"ya2_{s_}_{par}" for s_ in range(SB2)]
                YB = [f"yb2_{s_}_{par}" for s_ in range(SB2)]
                for m in range(8):
                    bga, bgb_ = mbank(), mbank()
                    for c in range(8):
                        MM(ps[bga][:, 0:TB2], wg[:, c, m * 128:(m + 1) * 128], h2[:, c, :], c == 0, c == 7, ["wg", H2K], [PK(bga)])
                    for c in range(8):
                        MM(ps[bgb_][:, 0:TB2], wg[:, c, 1024 + m * 128:1024 + (m + 1) * 128], h2[:, c, :], c == 0, c == 7, ["wg", H2K], [PK(bgb_)])
                    ACTF(gta[:], ps[bga][:, 0:TB2], ACT.Tanh, [PK(bga), "hbg"], ["gta"], scale=0.5, bias=hbg[:, m:m + 1])
                    ACTF(gtb[:], ps[bgb_][:, 0:TB2], ACT.Tanh, [PK(bgb_), "hbg"], ["gtb"], scale=0.5, bias=hbg[:, 8 + m:9 + m])
                    bpa, bpb = mbank(), mbank()
                    for c in range(4):
                        MM(ps[bpa][:, 0:TB2], wba[:, c, m * 128:(m + 1) * 128], ya2[:, c, :, :], c == 0, c == 3, ["wba"] + YA, [PK(bpa)])
                    for c in range(4):
                        MM(ps[bpb][:, 0:TB2], wbb[:, c, m * 128:(m + 1) * 128], yb2[:, c, :, :], c == 0, c == 3, ["wbb"] + YB, [PK(bpb)])
                    STT(ua[:], gta[:], 1.0, ps[bpa][:, 0:TB2], ALU.add, ALU.mult, ["gta", PK(bpa)], ["ua"])
                    STT(ub[:], gtb[:], 1.0, ps[bpb][:, 0:TB2], ALU.add, ALU.mult, ["gtb", PK(bpb)], ["ub"])
                    TT("pool", m2T[:, m, :], ua[:], ub[:], ALU.add, ["ua", "ub"], ["m2T"])
                    yield
                for s_ in range(SB2):
                    xp = s_ % 2
                    for half in range(2):
                        bo = mbank()
                        for m in range(8):
                            MM(ps[bo][:, :], m2T[:, m, s_ * 128:(s_ + 1) * 128], wo[:, m, half * 512:(half + 1) * 512], m == 0, m == 7,
                               ["m2T", "wo"], [PK(bo)])
                        TT("dve", x1t[xp][:, half * 512:(half + 1) * 512], ps[bo][:, :], xt[:, s_, half * 512:(half + 1) * 512], ALU.add,
                           [PK(bo), f"xt{s_}_{par}"], [f"x1t{xp}"])
                        yield
                    DMA("sp", out_d[t * TB2 + s_ * 128:t * TB2 + (s_ + 1) * 128, :], x1t[xp][:], [f"x1t{xp}"], (), f"st_x1t{xp}")

            run_all(a2_prep(0))
            for t in range(NT2):
                if t + 1 < NT2:
                    interleave(a2_main(t), a2_prep(t + 1), 1, 1)
                else:
                    run_all(a2_main(t))
            S.barrier()
            S.emit()

        if stop_after == "A2":
            return nc

        TB3 = min(256, S_len)
        NT3 = S_len // TB3
        SB3 = TB3 // 128
        with ExitStack() as e3:
            w1 = sb(e3, "w1", [128, 8, DFF], BF16)
            w3 = sb(e3, "w3", [128, 8, DFF], BF16)
            w2 = sb(e3, "w2", [128, NFF, D], BF16)
            xt3 = [sb(e3, f"xtb_{i}", [128, SB3, D]) for i in range(2)]
            xn3 = sb(e3, "xn2b", [128, D])
            ssb3 = sb(e3, "ssbb", [128, 4])
            h23 = [sb(e3, f"h2b_{i}", [128, 8, TB3], BF16) for i in range(2)]
            sg = [sb(e3, f"sg{i}", [128, TB3]) for i in range(2)]
            aT = sb(e3, "aT", [128, NFF, TB3], BF16)
            stg3 = [sb(e3, f"stg3_{i}", [128, D]) for i in range(2)]
            NCB = 4
            CBW = DFF // NCB
            def b_wload(cb):
                cs = slice(cb * CBW, (cb + 1) * CBW)
                for c in range(8):
                    DMA("pool", w1[:, c, cs], w1_d[c * 128:(c + 1) * 128, cs], (), [f"w1_{cb}_{c}"], f"ld_w1_{cb}")
                    DMA("pool", w3[:, c, cs], w3_d[c * 128:(c + 1) * 128, cs], (), [f"w3_{cb}_{c}"], f"ld_w3_{cb}")

            def w_keys(nm, f):
                lo, hi = f * 128, (f + 1) * 128 - 1
                return [f"{nm}_{cb}_{c}" for cb in sorted({lo // CBW, hi // CBW}) for c in range(8)]

            def b_w2():
                for f in range(NFF):
                    sp_ = f % 2
                    DMA("sp", stg3[sp_][:], w2_d[f * 128:(f + 1) * 128, :], (), [f"stg3_{sp_}"], f"ld_stg3_{sp_}")
                    TT("dve", w2[:, f, :], stg3[sp_][:], g2_bc[:], ALU.mult, [f"stg3_{sp_}", "g2_bc"], ["w2"])
                    yield

            def b_prep(t):
                par = t % 2
                xt, h2 = xt3[par], h23[par]
                for s_ in range(SB3):
                    DMA("sp", xt[:, s_, :], out_d[t * TB3 + s_ * 128:t * TB3 + (s_ + 1) * 128, :], (), [f"xtb{s_}_{par}"], f"ld_xtb{s_}_{par}")
                yield
                for s_ in range(SB3):
                    yield from norm_gen(xt[:, s_, :], h2, gs2, 24, f"xtb{s_}_{par}", f"h2b_{par}", s_, xn3, ssb3, "B")

            def b_main(t):
                par = t % 2
                xt, h2 = xt3[par], h23[par]
                H2K = f"h2b_{par}"
                for f in range(NFF):
                    bg1, bu1 = mbank(), mbank()
                    for c in range(8):
                        MM(ps[bg1][:, 0:TB3], w1[:, c, f * 128:(f + 1) * 128], h2[:, c, :], c == 0, c == 7, w_keys("w1", f) + [H2K], [PK(bg1)])
                    for c in range(8):
                        MM(ps[bu1][:, 0:TB3], w3[:, c, f * 128:(f + 1) * 128], h2[:, c, :], c == 0, c == 7, w_keys("w3", f) + [H2K], [PK(bu1)])
                    ACTF(sg[f % 2][:], ps[bg1][:, 0:TB3], ACT.Silu, [PK(bg1)], [f"sg{f % 2}"])
                    TT("dve", aT[:, f, :], sg[f % 2][:], ps[bu1][:, 0:TB3], ALU.mult, [f"sg{f % 2}", PK(bu1)], ["aT"])
                    if f % 2 == 1:
                        yield
                for s_ in range(SB3):
                    for half in range(2):
                        bo = mbank()
                        for f in range(NFF):
                            MM(ps[bo][:, :], aT[:, f, s_ * 128:(s_ + 1) * 128], w2[:, f, half * 512:(half + 1) * 512], f == 0, f == NFF - 1,
                               ["aT", "w2"], [PK(bo)])
                        TT("dve", xt[:, s_, half * 512:(half + 1) * 512], ps[bo][:, :], xt[:, s_, half * 512:(half + 1) * 512], ALU.add,
                           [PK(bo), f"xtb{s_}_{par}"], [f"xtb{s_}_{par}"])
                        yield
                    DMA("sp", out_d[t * TB3 + s_ * 128:t * TB3 + (s_ + 1) * 128, :], xt[:, s_, :], [f"xtb{s_}_{par}"], (), f"st_ot{s_}_{par}")

            b_wload(0)
            run_all(b_prep(0))
            for cb in range(1, NCB):
                b_wload(cb)

            def chain2(*gs):
                for g in gs:
                    yield from g

            for t in range(NT3):
                if t == 0 and NT3 > 1:
                    interleave(b_main(0), chain2(b_w2(), b_prep(1)), 1, 2)
                elif t == 0:
                    run_all(b_w2())
                    run_all(b_main(0))
                elif t + 1 < NT3:
                    interleave(b_main(t), b_prep(t + 1), 2, 1)
                else:
                    run_all(b_main(t))
            S.barrier()
            S.emit()
    return nc


def _consts():
    j = np.arange(64)[:, None]
    t = np.arange(64)[None, :]
    su = (j < t).astype(np.float32)
    ui = (j <= t).astype(np.float32)
    mask1 = np.concatenate([su, ui], axis=1)
    masksl = (t < j).astype(np.float32)
    kk = np.arange(128)[:, None]
    qq = np.arange(128)[None, :]
    amask = np.stack([(kk > qq), (kk <= qq)], axis=1).astype(np.float32)
    bones = np.zeros((128, 128), np.float32)
    bones[:64, :64] = 1.0
    bones[64:, 64:] = 1.0
    hind = np.zeros((128, 2), np.float32)
    hind[:64, 0] = 1.0
    hind[64:, 1] = 1.0
    half = 8
    invf = (np.float32(500000.0) ** (-np.arange(half, dtype=np.float32) / np.float32(half))).astype(np.float32)[None, :]
    return dict(ident=np.eye(128, dtype=np.float32), mask1=mask1, masksl=masksl, amask=amask, bones=bones, hind=hind, invf=invf)


def _pp(v, k):
    return np.ascontiguousarray(np.asarray(v, np.float32).reshape(k, 128).T)


def make_in_maps(inputs, S_len=4096, cores=None):
    f = lambda a: np.ascontiguousarray(np.asarray(a, np.float32))
    NB = S_len // 128
    shared = dict(
        ada_w=f(inputs["ada_w"][0]), ada_bT=_pp(inputs["ada_b"][0], 48),
        n1g=_pp(inputs["norm1_gain"][0], 8), n2g=_pp(inputs["norm2_gain"][0], 8),
        w_in=f(inputs["w_in"][0]), mu=f(inputs["tshift_mu"][0])[None, :],
        w0T=_pp(inputs["decay_w0"][0], 4), a0T=_pp(inputs["iclr_a0"][0], 4),
        kkT=_pp(inputs["k_k"][0], 4), kaT=_pp(inputs["k_a"][0], 4), rkT=_pp(np.asarray(inputs["r_k"][0]).reshape(-1), 4),
        lora_up=f(np.concatenate([inputs["decay_up"][0], inputs["iclr_up"][0]], axis=0)),
        gate_up=f(inputs["gate_up"][0]),
        lnx_g=f(inputs["lnx_gain"][0])[None, :], lnx_b=f(inputs["lnx_bias"][0])[None, :],
        qkg=f(np.concatenate([np.tile(np.asarray(inputs["q_norm_gain"][0]), 8), np.tile(np.asarray(inputs["k_norm_gain"][0]), 2)]))[None, :],
        sinks=f(inputs["attn_sinks"][0])[None, :],
        bgbT=_pp(inputs["branch_gate_b"][0], 16),
        wba=f(inputs["w_branch_a"][0]), wbb=f(inputs["w_branch_b"][0]), w_out=f(inputs["w_out"][0]),
        w1=f(inputs["ffn_w1"][0]), w3=f(inputs["ffn_w3"][0]), w2=f(inputs["ffn_w2"][0]),
    )
    shared.update(_consts())
    maps = []
    cores = range(8) if cores is None else cores
    for b in cores:
        m = dict(shared)
        m["x"] = f(inputs["x"][b, :S_len])
        cT = np.zeros((128, 8, 2), np.float32)
        cT[:, :, 0] = _pp(inputs["c"][b], 8)
        m["cT"] = cT
        m["pos"] = np.ascontiguousarray(np.asarray(inputs["positions"][b, :S_len], np.int32).reshape(NB, 128).T)
        maps.append(m)
    return maps


_NC_CACHE = {}


def kernel(**inputs):
    if "nc" not in _NC_CACHE:
        _NC_CACHE["nc"] = build(4096, "B", False)
    nc = _NC_CACHE["nc"]
    maps = make_in_maps(inputs, 4096)
    res = run_bass_kernel_spmd(nc, maps, core_ids=list(range(8)))
    out = np.stack([np.asarray(r["out"], np.float32) for r in res.results], axis=0)
    return out
```
